# Optimizing a Trainium2 kernel written in Bass

```python
import jax, jax.numpy as jnp
from jax import lax
import numpy as np

D_MODEL = 1024
BATCH = 4
SEQ = 8192
DEPTH = 1

CHUNK = 64
N_META = 16
D_MIX = D_MODEL
D_ATTN = D_MIX // 2
D_CONV = D_MIX - D_ATTN
ATTN_HEADS = 8
HEAD_DIM = D_ATTN // ATTN_HEADS
CONV_WIDTH = 3
D_FF = 2816
Q_BLOCK = 128
EPS = 1e-6
MIX_IN_SIZES = (D_ATTN, D_ATTN, D_ATTN, ATTN_HEADS, D_CONV, D_CONV, D_CONV)
MIX_IN_WIDTH = sum(MIX_IN_SIZES)
MIX_IN_SPLITS = tuple(int(s) for s in np.cumsum(MIX_IN_SIZES)[:-1])

kernel_name = "hymba_fox_shortconv_macaron_block"


def rmsnorm(x, g):
    xf = x.astype(jnp.float32)
    y = xf * lax.rsqrt(jnp.mean(xf * xf, axis=-1, keepdims=True) + EPS)
    return (y * g.astype(jnp.float32)).astype(x.dtype)


def swiglu(h, w_in, w_out):
    gate, up = jnp.split(h @ w_in, 2, axis=-1)
    return (jax.nn.silu(gate) * up) @ w_out


def forgetting_attention(q, k, v, f_logit, b_forget, q_norm, k_norm):
    b, l, _ = q.shape
    q = rmsnorm(q.reshape(b, l, ATTN_HEADS, HEAD_DIM), q_norm)
    k = rmsnorm(k.reshape(b, l, ATTN_HEADS, HEAD_DIM), k_norm)
    v = v.reshape(b, l, ATTN_HEADS, HEAD_DIM)
    log_f = jax.nn.log_sigmoid((f_logit + b_forget).astype(jnp.float32))
    cum = jnp.cumsum(log_f, axis=1)
    lp = -(-l // Q_BLOCK) * Q_BLOCK
    pad = lp - l
    nb = lp // Q_BLOCK
    pad4 = ((0, 0), (0, pad), (0, 0), (0, 0))
    qh = jnp.pad(q, pad4).transpose(0, 2, 1, 3).astype(jnp.float32)
    kh = jnp.pad(k, pad4).transpose(0, 2, 1, 3).astype(jnp.float32)
    vh = jnp.pad(v, pad4).transpose(0, 2, 1, 3)
    cum_h = jnp.pad(cum, ((0, 0), (0, pad), (0, 0))).transpose(0, 2, 1)
    q_blocks = qh.reshape(b, ATTN_HEADS, nb, Q_BLOCK, HEAD_DIM).transpose(2, 0, 1, 3, 4)
    cq_blocks = cum_h.reshape(b, ATTN_HEADS, nb, Q_BLOCK).transpose(2, 0, 1, 3)
    kpos = jnp.arange(lp)
    scale = HEAD_DIM ** -0.5

    def one_block(args):
        qb, cqb, blk = args
        s = jnp.einsum('bhqd,bhkd->bhqk', qb, kh) * scale \
            + cqb[..., :, None] - cum_h[..., None, :]
        qpos = blk * Q_BLOCK + jnp.arange(Q_BLOCK)
        s = jnp.where(kpos[None, :] <= qpos[:, None], s, -jnp.inf)
        p = jax.nn.softmax(s, axis=-1)
        return jnp.einsum('bhqk,bhkd->bhqd', p.astype(vh.dtype), vh)

    o = lax.map(one_block, (q_blocks, cq_blocks, jnp.arange(nb)))
    o = o.transpose(1, 0, 3, 2, 4).reshape(b, lp, D_ATTN)[:, :l]
    return o.astype(q.dtype)


def gated_short_conv(gate_b, gate_c, u, conv_w):
    z = gate_c * u
    l = z.shape[1]
    zp = jnp.pad(z, ((0, 0), (CONV_WIDTH - 1, 0), (0, 0)))
    y = sum(zp[:, i:i + l] * conv_w[i] for i in range(CONV_WIDTH))
    return gate_b * y


def token_mixing(h, w_mix_in, b_forget, q_norm, k_norm, conv_w, attn_out_norm, conv_out_norm, w_mix_out):
    proj = h @ w_mix_in
    q, k, v, f_logit, gate_b, gate_c, u = jnp.split(proj, MIX_IN_SPLITS, axis=-1)
    o_attn = forgetting_attention(q, k, v, f_logit, b_forget, q_norm, k_norm)
    o_conv = gated_short_conv(gate_b, gate_c, u, conv_w)
    merged = jnp.concatenate([rmsnorm(o_attn, attn_out_norm), rmsnorm(o_conv, conv_out_norm)], axis=-1)
    return merged @ w_mix_out


def setup_inputs(seed: int = 0) -> dict:
    key = jax.random.key(seed)
    ks = jax.random.split(key, 20)
    f32 = jnp.float32
    nrm = lambda k, shape, s: jax.random.normal(k, shape, f32) * s
    gain = lambda k, shape: 1.0 + 0.02 * jax.random.normal(k, shape, f32)
    L = DEPTH
    return {
        "x": jax.random.normal(ks[0], (BATCH, SEQ, D_MODEL), f32),
        "meta_tokens": nrm(ks[1], (N_META, D_MODEL), 1.0),
        "ffn1_norm": gain(ks[2], (L, D_MODEL)),
        "ffn1_w_in": nrm(ks[3], (L, D_MODEL, 2 * D_FF), D_MODEL ** -0.5),
        "ffn1_w_out": nrm(ks[4], (L, D_FF, D_MODEL), D_FF ** -0.5),
        "mix_norm": gain(ks[5], (L, D_MODEL)),
        "w_mix_in": nrm(ks[6], (L, D_MODEL, MIX_IN_WIDTH), D_MODEL ** -0.5),
        "b_forget": nrm(ks[7], (L, ATTN_HEADS), 0.1),
        "q_norm": gain(ks[8], (L, HEAD_DIM)),
        "k_norm": gain(ks[9], (L, HEAD_DIM)),
        "conv_w": nrm(ks[10], (L, CONV_WIDTH, D_CONV), CONV_WIDTH ** -0.5),
        "attn_out_norm": gain(ks[11], (L, D_ATTN)),
        "conv_out_norm": gain(ks[12], (L, D_CONV)),
        "w_mix_out": nrm(ks[13], (L, D_MIX, D_MODEL), D_MIX ** -0.5),
        "ffn2_norm": gain(ks[14], (L, D_MODEL)),
        "ffn2_w_in": nrm(ks[15], (L, D_MODEL, 2 * D_FF), D_MODEL ** -0.5),
        "ffn2_w_out": nrm(ks[16], (L, D_FF, D_MODEL), D_FF ** -0.5),
        "final_norm": gain(ks[17], (L, D_MODEL)),
    }


def reference(x, meta_tokens, ffn1_norm, ffn1_w_in, ffn1_w_out, mix_norm, w_mix_in, b_forget,
              q_norm, k_norm, conv_w, attn_out_norm, conv_out_norm, w_mix_out,
              ffn2_norm, ffn2_w_in, ffn2_w_out, final_norm):
    b = x.shape[0]
    meta = jnp.broadcast_to(meta_tokens[None].astype(x.dtype), (b, N_META, D_MODEL))
    h = jnp.concatenate([meta, x], axis=1)
    for i in range(DEPTH):
        h = h + 0.5 * swiglu(rmsnorm(h, ffn1_norm[i]), ffn1_w_in[i], ffn1_w_out[i])
        h = h + token_mixing(rmsnorm(h, mix_norm[i]), w_mix_in[i], b_forget[i], q_norm[i], k_norm[i],
                             conv_w[i], attn_out_norm[i], conv_out_norm[i], w_mix_out[i])
        h = h + 0.5 * swiglu(rmsnorm(h, ffn2_norm[i]), ffn2_w_in[i], ffn2_w_out[i])
        h = rmsnorm(h, final_norm[i])
    return h[:, N_META:]
```

```python
import numpy as np
from contextlib import ExitStack
import concourse.bass as bass
import concourse.mybir as mybir
from concourse.bass_utils import run_bass_kernel_spmd

F32 = mybir.dt.float32
BF16 = mybir.dt.bfloat16
AF = mybir.ActivationFunctionType
ALU = mybir.AluOpType
AX = mybir.AxisListType

ENGS = ["pe", "act", "dve", "pool", "sp"]
D = 1024
DFF = 2816
NF = 22
NOWN = 32
NBLK = 65
EPS = 1e-6
NEG = -30000.0
VD = 72
VSZ = NBLK * 8 * VD + 128
DEBUG = False
PHASES = "ABCDE"


class Op:
    __slots__ = ("eng", "fn", "deps", "odeps", "marked", "seq", "key", "total", "idx", "cost", "bgroup")

    def __init__(self, eng, fn):
        self.eng = eng
        self.fn = fn
        self.deps = set()
        self.odeps = set()
        self.marked = False
        self.seq = 0
        self.key = None
        self.total = 0
        self.idx = 0
        self.cost = 0.3
        self.bgroup = None


DEF_COST = {"pe": 0.5, "act": 0.4, "dve": 0.4, "pool": 0.3, "sp": 0.05}


class Sched:
    def __init__(self):
        self.ops = {e: [] for e in ENGS}
        self.last_w = {}
        self.readers = {}
        self.dma_tot = {}
        self.last_dma = {}
        self.n = 0
        self.fixed = True
        self.prev = {}
        self.cur_bar = {}
        self.since_bar = {e: [] for e in ENGS}
        self.bgroups = []

    def _track(self, o, reads, writes):
        deps = o.deps
        for r in reads:
            w = self.last_w.get(r)
            if w is not None:
                deps.add(w)
        for w_ in writes:
            w = self.last_w.get(w_)
            if w is not None:
                deps.add(w)
            for rd in self.readers.get(w_, ()):
                deps.add(rd)
        deps.discard(o)
        for r in reads:
            self.readers.setdefault(r, []).append(o)
        for w_ in writes:
            self.last_w[w_] = o
            self.readers[w_] = []

    def _order(self, o):
        o.idx = self.n
        self.n += 1
        b = self.cur_bar.get(o.eng)
        if b is not None:
            o.odeps.add(b)
        if self.fixed:
            p = self.prev.get(o.eng)
            if p is not None:
                o.odeps.add(p)
        self.prev[o.eng] = o
        self.since_bar[o.eng].append(o)
        self.ops[o.eng].append(o)

    def op(self, eng, fn, reads=(), writes=(), cost=None):
        o = Op(eng, fn)
        o.cost = DEF_COST[eng] if cost is None else cost
        self._track(o, reads, writes)
        self._order(o)
        return o

    def dma(self, eng, key, fn, reads=(), writes=(), n=1, cost=3.0):
        o = Op(eng, fn)
        o.key = key
        o.cost = cost
        self._track(o, reads, writes)
        prev = self.last_dma.get(key)
        if prev is not None:
            o.deps.add(prev)
        self.dma_tot[key] = self.dma_tot.get(key, 0) + 16 * n
        o.total = self.dma_tot[key]
        self.last_dma[key] = o
        self._order(o)
        return o

    def barrier(self):
        dmas = list(self.last_dma.values())
        grp = {}
        for e in ENGS:
            o = Op(e, None)
            o.cost = 0.0
            for d in dmas:
                o.deps.add(d)
            for p in self.since_bar[e]:
                o.odeps.add(p)
            o.idx = self.n
            self.n += 1
            self.ops[e].append(o)
            self.since_bar[e] = []
            self.cur_bar[e] = o
            self.prev[e] = o
            o.bgroup = grp
            grp[e] = o
        self.bgroups.append(grp)
        self.last_w = {}
        self.readers = {}

    def _schedule(self):
        allops = [o for e in ENGS for o in self.ops[e]]
        for grp in self.bgroups:
            for e, b in grp.items():
                for f, bf in grp.items():
                    if f != e:
                        b.odeps |= bf.odeps
        succ = {}
        nun = {}
        for o in allops:
            ds = o.deps | o.odeps
            ds.discard(o)
            nun[o] = len(ds)
            for d in ds:
                succ.setdefault(d, []).append(o)
        ready = {}
        finish = {}
        avail = {e: [] for e in ENGS}
        for o in allops:
            if nun[o] == 0:
                ready[o] = 0.0
                avail[o.eng].append(o)
        t_eng = {e: 0.0 for e in ENGS}
        order = {e: [] for e in ENGS}
        left = len(allops)
        while left:
            best = None
            for e in ENGS:
                te = t_eng[e]
                for o in avail[e]:
                    st = ready[o] if ready[o] > te else te
                    k = (st, o.idx)
                    if best is None or k < best[0]:
                        best = (k, o)
            (st, _), o = best
            e = o.eng
            avail[e].remove(o)
            if o.key is not None:
                t_eng[e] = st + 0.06
                fin = st + o.cost
            else:
                fin = st + o.cost
                t_eng[e] = fin
            finish[o] = fin
            order[e].append(o)
            left -= 1
            for sname in succ.get(o, ()):
                nun[sname] -= 1
                r = fin + (0.15 if sname.eng != e else 0.0)
                if r > ready.get(sname, 0.0):
                    ready[sname] = r
                if nun[sname] == 0:
                    avail[sname.eng].append(sname)
        self.ops = order
        self.est_time = max(finish.values()) if finish else 0.0

    def finalize(self):
        self._schedule()
        for grp in self.bgroups:
            lasts = []
            for f, bf in grp.items():
                lst = self.ops[f]
                i = lst.index(bf) - 1
                while i >= 0 and lst[i].fn is None:
                    i -= 1
                if i >= 0:
                    lasts.append(lst[i])
            for e, b in grp.items():
                for l in lasts:
                    if l.eng != e:
                        b.deps.add(l)
        for e in ENGS:
            for o in self.ops[e]:
                for d in o.deps:
                    if d.key is None and d.fn is not None:
                        if d.eng == o.eng and d.eng == "pe":
                            continue
                        d.marked = True
        for e in ENGS:
            c = 0
            for o in self.ops[e]:
                if o.key is None and o.marked:
                    c += 1
                    o.seq = c

    def emit(self, eng, h, sems, dsems):
        waited = {}
        for o in self.ops[eng]:
            need = {}
            for d in o.deps:
                if d.key is not None:
                    k = ("d", d.key)
                    v = d.total
                else:
                    if not d.marked:
                        continue
                    if d.eng == eng and eng == "pe":
                        continue
                    k = ("e", d.eng)
                    v = d.seq
                if v > need.get(k, 0):
                    need[k] = v
            for k, v in need.items():
                if waited.get(k, 0) >= v:
                    continue
                waited[k] = v
                s = dsems[k[1]] if k[0] == "d" else sems[k[1]]
                h.wait_ge(s, v)
            if o.fn is None:
                continue
            if o.key is not None:
                o.fn(h, dsems[o.key])
            else:
                ins = o.fn(h)
                if o.marked:
                    ins.then_inc(sems[eng], 1)


def run_sched(nc, S, es):
    S.finalize()
    sems = {e: es.enter_context(nc.semaphore("s_" + e)) for e in ENGS}
    dsems = {k: es.enter_context(nc.semaphore("d_%d" % i)) for i, k in enumerate(S.dma_tot)}
    with nc.Block() as block:
        @block.tensor
        def _(h):
            S.emit("pe", h, sems, dsems)

        @block.scalar
        def _(h):
            S.emit("act", h, sems, dsems)

        @block.vector
        def _(h):
            S.emit("dve", h, sems, dsems)

        @block.gpsimd
        def _(h):
            S.emit("pool", h, sems, dsems)

        @block.sync
        def _(h):
            S.emit("sp", h, sems, dsems)


NBF = 87040
NF32 = 9472
CB = NBF - 384
C_TRI, C_ONES, C_MA, C_MB, C_SEL, C_BFG, C_CW, C_AG, C_CG = 0, 128, 256, 384, 512, 514, 522, 534, 538
C_IDF = 544
NCF = 672
SSC = 672
SRUN = 704
F32_FREE = 736


def build_nc():
    nc = bass.Bass("TRN2", target_bir_lowering=False)
    dt = lambda name, shape, ty=F32: nc.dram_tensor(name, shape, ty, kind="ExternalInput").ap()
    xs = dt("xs", [NBLK, 128, D])
    w1i = dt("w1i", [D, 2 * DFF])
    w1o = dt("w1o", [DFF, D])
    w2i = dt("w2i", [D, 2 * DFF])
    w2o = dt("w2o", [DFF, D])
    wmi = dt("wmi", [D, 3080])
    wmo = dt("wmo", [D, D])
    g1 = dt("g1", [128, D])
    gm = dt("gm", [128, D])
    g2 = dt("g2", [128, D])
    gf = dt("gf", [128, D])
    gqk = dt("gqk", [128, 1024])
    cf = dt("cf", [128, NCF])
    cb = dt("cb", [128, 384])
    out_d = nc.dram_tensor("out", [NOWN, 128, D], F32, kind="ExternalOutput").ap()
    kind = "ExternalOutput" if DEBUG else "Internal"
    scr = lambda name, shape, ty: nc.dram_tensor(name, shape, ty, kind=kind).ap()
    h1s = scr("h1s", [NBLK, 128, D], F32)
    h2s = scr("h2s", [NOWN, 128, D], F32)
    KTd = scr("KTd", [70, 8, NBLK * 128], BF16)
    QTd = scr("QTd", [70, 8, NOWN * 128], BF16)
    OAd = scr("OAd", [NOWN, 128, 512], BF16)
    OCTd = scr("OCTd", [NOWN, 128, 512], BF16)
    if DEBUG:
        Vd = scr("Vd", [128, VSZ], BF16)
        sscd = scr("sscd", [128, 32], F32)

    with ExitStack() as es:
        abf = es.enter_context(nc.sbuf_tensor("abf", [128, NBF], BF16))
        af = es.enter_context(nc.sbuf_tensor("af", [128, NF32], F32))
        big = [es.enter_context(nc.psum_tensor("pbig%d" % i, [128, 1024], F32)) for i in range(4)]
        banks = [big[i // 2][:, (i % 2) * 512:(i % 2) * 512 + 512] for i in range(8)]
        S = Sched()
        op = S.op

        ident = abf[:, CB:CB + 128]
        trimask = abf[:, CB + 128:CB + 256]
        maskP = abf[:, CB + 256:CB + 384]
        cfa = af[:, 0:NCF]
        TRI = af[:, C_TRI:C_TRI + 128]
        ONES = af[:, C_ONES:C_ONES + 128]
        MA = af[:, C_MA:C_MA + 128]
        MB = af[:, C_MB:C_MB + 128]
        ssc = af[:, SSC:SSC + 32]
        srun = af[:, SRUN:SRUN + 8]

        def load_consts():
            S.dma("sp", "cf", lambda e, s: e.dma_start(out=cfa, in_=cf).then_inc(s, 16), writes=["cf"])
            S.dma("pool", "cb", lambda e, s: e.dma_start(out=abf[:, CB:CB + 384], in_=cb).then_inc(s, 16), writes=["cb"])

        def rms_rstd(tag, src, width, junk, stat, extra_bias=None):
            op("pool", lambda e: e.memset(stat[:, 0:1], 0.0), reads=[], writes=[tag + "ss"])
            op("act", lambda e: e.activation(out=junk, in_=src, func=AF.Square, accum_out=stat[:, 0:1]),
               reads=[tag + "src"], writes=[tag + "junk", tag + "ss"])
            op("act", lambda e: e.activation(out=stat[:, 1:2], in_=stat[:, 0:1], func=AF.Ln, scale=1.0 / width, bias=EPS),
               reads=[tag + "ss"], writes=[tag + "ln"])
            op("act", lambda e: e.activation(out=stat[:, 2:3], in_=stat[:, 1:2], func=AF.Exp, scale=-0.5),
               reads=[tag + "ln"], writes=[tag + "rstd"])

        def norm_transpose(tag, xt, xres, gain, gres, xn, xnres, stat, dstT, dres, col0, tb=0):
            op("pool", lambda e: e.memset(stat[:, 0:1], 0.0), writes=[tag + "ss"])
            op("act", lambda e: e.activation(out=xn, in_=xt, func=AF.Square, accum_out=stat[:, 0:1]),
               reads=[xres], writes=[xnres, tag + "ss"])
            op("act", lambda e: e.activation(out=stat[:, 1:2], in_=stat[:, 0:1], func=AF.Ln, scale=1.0 / D, bias=EPS),
               reads=[tag + "ss"], writes=[tag + "ln"])
            op("act", lambda e: e.activation(out=stat[:, 2:3], in_=stat[:, 1:2], func=AF.Exp, scale=-0.5),
               reads=[tag + "ln"], writes=[tag + "rstd"])
            op("dve", lambda e: e.scalar_tensor_tensor(out=xn, in0=xt, scalar=stat[:, 2:3], in1=gain,
                                                       op0=ALU.mult, op1=ALU.mult),
               reads=[xres, tag + "rstd", gres], writes=[xnres])
            pt = banks[tb].bitcast(BF16)

            def tr(e):
                for c in range(8):
                    i = e.transpose(out=pt[:, c * 128:(c + 1) * 128], in_=xn[:, c * 128:(c + 1) * 128], identity=ident)
                return i
            op("pe", tr, reads=[xnres, "cb"], writes=["bank%d" % tb])
            op("act", lambda e: e.activation(out=dstT[:, :, col0:col0 + 128],
                                             in_=pt.rearrange("p (c t) -> p c t", t=128), func=AF.Copy),
               reads=["bank%d" % tb], writes=[dres])

        W1 = abf[:, 0:8 * 5632].rearrange("p (c n) -> p c n", n=5632)
        W2 = abf[:, 45056:45056 + NF * D].rearrange("p (f n) -> p f n", n=D)
        HT = abf[:, 67584:67584 + NF * 256].rearrange("p (f n) -> p f n", n=256)
        XNTB = [abf[:, 73216 + i * 2048:73216 + (i + 1) * 2048].rearrange("p (c n) -> p c n", n=256) for i in range(2)]
        XN = [abf[:, 77312 + i * 1024:77312 + (i + 1) * 1024] for i in range(2)]
        JUNK = abf[:, 79360:79360 + 1024]
        NXT = 5
        XT = [af[:, F32_FREE + i * 1024:F32_FREE + (i + 1) * 1024] for i in range(NXT)]
        o_ = F32_FREE + NXT * 1024
        SG = [af[:, o_ + i * 256:o_ + (i + 1) * 256] for i in range(2)]
        o_ += 512
        GAIN = af[:, o_:o_ + 1024]
        GAIN2 = af[:, o_ + 1024:o_ + 2048]
        o_ += 2048
        STAT = [af[:, o_ + i * 4:o_ + (i + 1) * 4] for i in range(8)]
        o_ += 32
        assert o_ <= NF32

        def load_ffn_weights(tag, wi, wo):
            for q in range(4):
                def f(e, s, q=q):
                    for c in range(8):
                        e.dma_start(out=W1[:, c, q * 1408:(q + 1) * 1408],
                                    in_=wi[c * 128:(c + 1) * 128, q * 1408:(q + 1) * 1408]).then_inc(s, 16)
                S.dma("pool", tag + "w1q%d" % q, f, writes=["W1q%d" % q], n=8)
            for hlf in range(2):
                def f(e, s, hlf=hlf):
                    e.dma_start(out=W2[:, hlf * 11:(hlf + 1) * 11, :],
                                in_=wo[hlf * 1408:(hlf + 1) * 1408, :].rearrange("(f p) n -> p f n", p=128)).then_inc(s, 16)
                S.dma("pool", tag + "w2h%d" % hlf, f, writes=["W2h%d" % hlf])

        def ffn_phase(tag, groups, src_d, gain_d, finish):
            S.dma("sp", tag + "gain", lambda e, s: e.dma_start(out=GAIN, in_=gain_d).then_inc(s, 16), writes=["GAIN"])
            cnt = [0]
            info = {}

            def prep_a(g):
                tiles = []
                for j, blk in enumerate(groups[g]):
                    k = cnt[0] % NXT
                    cnt[0] += 1
                    xt, xr = XT[k], "XT%d" % k
                    tiles.append((xt, xr, k))
                    S.dma("sp", "ldx%d" % k, lambda e, s, xt=xt, blk=blk: e.dma_start(out=xt, in_=src_d[blk]).then_inc(s, 16),
                          reads=[tag + "src%d" % blk], writes=[xr])
                    st, xn, tg = STAT[j], XN[j], "f%d" % j
                    op("pool", lambda e, st=st: e.memset(st[:, 0:1], 0.0), writes=[tg + "ss"])
                    op("act", lambda e, st=st, xn=xn, xt=xt: e.activation(out=xn, in_=xt, func=AF.Square, accum_out=st[:, 0:1]),
                       reads=[xr], writes=["XN%d" % j, tg + "ss"])
                    op("act", lambda e, st=st: e.activation(out=st[:, 1:2], in_=st[:, 0:1], func=AF.Ln, scale=1.0 / D, bias=EPS),
                       reads=[tg + "ss"], writes=[tg + "ln"])
                    op("act", lambda e, st=st: e.activation(out=st[:, 2:3], in_=st[:, 1:2], func=AF.Exp, scale=-0.5),
                       reads=[tg + "ln"], writes=[tg + "rstd"])
                    op("dve", lambda e, st=st, xn=xn, xt=xt: e.scalar_tensor_tensor(out=xn, in0=xt, scalar=st[:, 2:3], in1=GAIN,
                                                                                   op0=ALU.mult, op1=ALU.mult),
                       reads=[xr, tg + "rstd", "GAIN"], writes=["XN%d" % j])
                info[g] = tiles

            def prep_b(g):
                xb = XNTB[g % 2]
                pt = banks[0].bitcast(BF16)
                for j, blk in enumerate(groups[g]):
                    xn = XN[j]

                    def tr(e, xn=xn):
                        for c in range(8):
                            i = e.transpose(out=pt[:, c * 128:(c + 1) * 128], in_=xn[:, c * 128:(c + 1) * 128], identity=ident)
                        return i
                    op("pe", tr, reads=["XN%d" % j, "cb"], writes=["bank0"])
                    op("act", lambda e, j=j: e.activation(out=xb[:, :, j * 128:(j + 1) * 128],
                                                          in_=pt.rearrange("p (c t) -> p c t", t=128), func=AF.Copy),
                       reads=["bank0"], writes=["XNT%d" % (g % 2)])

            prep_a(0)
            prep_b(0)
            for g, grp in enumerate(groups):
                N = 128 * len(grp)
                XNT = XNTB[g % 2]
                xres = "XNT%d" % (g % 2)
                tiles = info[g]
                for f in range(NF):
                    pg = banks[1 + 2 * (f % 2)]
                    pu = banks[2 + 2 * (f % 2)]
                    gq = (f * 128) // 1408
                    uq = (DFF + f * 128) // 1408

                    def mmg(e, f=f, pg=pg, N=N, XNT=XNT):
                        for c in range(8):
                            i = e.matmul(pg[:, 0:N], lhsT=W1[:, c, f * 128:(f + 1) * 128], rhs=XNT[:, c, 0:N],
                                         start=(c == 0), stop=(c == 7))
                        return i

                    def mmu(e, f=f, pu=pu, N=N, XNT=XNT):
                        for c in range(8):
                            i = e.matmul(pu[:, 0:N], lhsT=W1[:, c, DFF + f * 128:DFF + (f + 1) * 128], rhs=XNT[:, c, 0:N],
                                         start=(c == 0), stop=(c == 7))
                        return i
                    op("pe", mmg, reads=["W1q%d" % gq, xres], writes=["bank%d" % (1 + 2 * (f % 2))])
                    op("pe", mmu, reads=["W1q%d" % uq, xres], writes=["bank%d" % (2 + 2 * (f % 2))])
                    sg = SG[f % 2]
                    op("act", lambda e, sg=sg, pg=pg, N=N: e.activation(out=sg[:, 0:N], in_=pg[:, 0:N], func=AF.Silu),
                       reads=["bank%d" % (1 + 2 * (f % 2))], writes=["SG%d" % (f % 2)])
                    op("dve", lambda e, sg=sg, pu=pu, f=f, N=N: e.tensor_tensor(out=HT[:, f, 0:N], in0=pu[:, 0:N], in1=sg[:, 0:N],
                                                                                 op=ALU.mult),
                       reads=["bank%d" % (2 + 2 * (f % 2)), "SG%d" % (f % 2)], writes=["HT%d" % f])
                    if f == 10 and g + 1 < len(groups):
                        prep_a(g + 1)
                for j, blk in enumerate(grp):
                    xt, xr, k = tiles[j]
                    for hlf in range(2):
                        po = banks[5 + hlf]

                        def mm2(e, j=j, hlf=hlf, po=po):
                            for f in range(NF):
                                i = e.matmul(po[:, 0:512], lhsT=HT[:, f, j * 128:(j + 1) * 128],
                                             rhs=W2[:, f, hlf * 512:(hlf + 1) * 512], start=(f == 0), stop=(f == NF - 1))
                            return i
                        op("pe", mm2, reads=["HT%d" % f for f in range(NF)] + ["W2h0", "W2h1"], writes=["bank%d" % (5 + hlf)])
                        op("dve", lambda e, xt=xt, po=po, hlf=hlf: e.scalar_tensor_tensor(
                            out=xt[:, hlf * 512:(hlf + 1) * 512], in0=po[:, 0:512], scalar=0.5,
                            in1=xt[:, hlf * 512:(hlf + 1) * 512], op0=ALU.mult, op1=ALU.add),
                           reads=["bank%d" % (5 + hlf), xr], writes=[xr])
                    finish(blk, xt, xr, k)
                    if j == 0 and g + 1 < len(groups):
                        prep_b(g + 1)
                if len(grp) == 1 and g + 1 < len(groups):
                    pass

        load_consts()
        if "A" in PHASES:
            load_ffn_weights("a", w1i, w1o)

            def finA(blk, xt, xr, k):
                S.dma("pool", "stx%d" % k, lambda e, s: e.dma_start(out=h1s[blk], in_=xt).then_inc(s, 16),
                      reads=[xr], writes=["h1s%d" % blk])
            groups = [[1 + 2 * g + j for j in range(2)] for g in range(32)] + [[0]]
            ffn_phase("a", groups, xs, g1, finA)
        S.barrier()

        VALL = abf[:, 0:VSZ - 128].rearrange("p (i h d) -> p i h d", h=8, d=VD)
        if "B" in PHASES:
            S.fixed = False
            bo = VSZ
            WM = abf[:, bo:bo + 8 * 3080].rearrange("p (c n) -> p c n", n=3080)
            bo += 8 * 3080
            XMs = [abf[:, bo + i * 1024:bo + (i + 1) * 1024] for i in range(2)]
            bo += 2048
            XMT = [abf[:, bo + i * 1024:bo + (i + 1) * 1024].rearrange("p (c n) -> p c n", n=128) for i in range(4)]
            bo += 4096
            KA = [abf[:, bo + i * 560:bo + (i + 1) * 560].rearrange("p (h d) -> p h d", d=70) for i in range(4)]
            bo += 4 * 560
            QAs = [abf[:, bo + i * 560:bo + (i + 1) * 560].rearrange("p (h d) -> p h d", d=70) for i in range(2)]
            bo += 2 * 560
            STG = [abf[0:70, bo + i * 1024:bo + (i + 1) * 1024].rearrange("p (h t) -> p h t", t=128) for i in range(3)]
            bo += 3 * 1024
            OCTT = [abf[:, bo + i * 512:bo + (i + 1) * 512].rearrange("p (j t) -> p j t", t=128) for i in range(2)]
            bo += 1024
            assert bo <= CB
            o = F32_FREE
            NH = 3
            H32 = [af[:, o + i * 1024:o + (i + 1) * 1024] for i in range(NH)]
            o += NH * 1024
            GMIX = af[:, o:o + 1024]
            o += 1024
            GQ = af[:, o:o + 512].rearrange("p (h d) -> p h d", d=64)
            GK = af[:, o + 512:o + 1024].rearrange("p (h d) -> p h d", d=64)
            GQK = af[:, o:o + 1024]
            o += 1024
            TMP = af[:, o:o + 512]
            o += 512
            ZOs = [af[:, o + i * 520:o + (i + 1) * 520].rearrange("p (j t) -> p j t", t=130) for i in range(2)]
            o += 1040
            ZPL = [af[:, o + i * 8:o + (i + 1) * 8].rearrange("p (j t) -> p j t", t=2) for i in range(4)]
            o += 32
            UC = af[:, o:o + 512].rearrange("p (j t) -> p j t", t=128)
            o += 512
            YC = [af[:, o + i * 128:o + (i + 1) * 128] for i in range(2)]
            o += 256
            OC32 = af[:, o:o + 512].rearrange("p (j t) -> p j t", t=128)
            OC32f = af[:, o:o + 512]
            o += 512
            SQ = af[:, o:o + 512]
            o += 512
            STB = [af[:, o + i * 4:o + (i + 1) * 4] for i in range(4)]
            o += 16
            SSH = [af[:, o + i * 8:o + (i + 1) * 8] for i in range(6)]
            o += 48
            SP32 = [af[:, o + i * 8:o + (i + 1) * 8] for i in range(4)]
            o += 32
            YFs = [af[:, o + i * 8:o + (i + 1) * 8] for i in range(4)]
            o += 32
            GRR = [af[:, o + i * 24:o + (i + 1) * 24] for i in range(4)]
            o += 96
            assert o <= NF32, o

            def f(e, s):
                for c in range(8):
                    e.dma_start(out=WM[:, c, :], in_=wmi[c * 128:(c + 1) * 128, :]).then_inc(s, 16)
            S.dma("pool", "wm", f, writes=["WM"], n=8, cost=40.0)
            S.dma("sp", "gmix", lambda e, s: e.dma_start(out=GMIX, in_=gm).then_inc(s, 16), writes=["GMIX"])
            S.dma("sp", "gqk", lambda e, s: e.dma_start(out=GQK, in_=gqk).then_inc(s, 16), writes=["GQK"])
            for i in range(4):
                op("pool", lambda e, i=i: e.memset(KA[i][:, :, 64:67], 1.0), writes=["KA%d" % i])
            for i in range(2):
                op("pool", lambda e, i=i: e.memset(QAs[i][:, :, 67:70], 1.0), writes=["QA%d" % i])
            op("pool", lambda e: e.memset(VALL[:, :, :, 64:65], 1.0), writes=["VALL"])
            op("pool", lambda e: e.memset(srun, 0.0), writes=["SRUN"])
            op("pool", lambda e: e.memset(ssc, 0.0), writes=["SSC"])

            stgc = [0]
            hcnt = [0]
            PB_T, PB_K, PB_V, PB_Q, PB_S, PB_C, PB_U, PB_X = 0, 1, 2, 3, 4, 5, 6, 7

            def headnorm(tag, src_ps, bres, dst, dres, gain, extra_ln_bias, ss):
                srcv = src_ps[:, 0:512].rearrange("p (h d) -> p h d", d=64)
                tv = TMP.rearrange("p (h d) -> p h d", d=64)
                op("act", lambda e: e.activation(out=TMP, in_=src_ps[:, 0:512], func=AF.Square),
                   reads=[bres], writes=["TMP"], cost=0.6)
                op("dve", lambda e: e.tensor_reduce(out=ss[:, 0:8], in_=tv, axis=AX.X, op=ALU.add),
                   reads=["TMP"], writes=[tag + "ss"], cost=0.7)
                op("act", lambda e: e.activation(out=ss[:, 0:8], in_=ss[:, 0:8], func=AF.Ln, scale=1.0 / 64, bias=EPS),
                   reads=[tag + "ss"], writes=[tag + "ss"], cost=0.25)
                op("act", lambda e: e.activation(out=ss[:, 0:8], in_=ss[:, 0:8], func=AF.Exp, scale=-0.5, bias=extra_ln_bias),
                   reads=[tag + "ss"], writes=[tag + "ss"], cost=0.25)
                op("dve", lambda e: e.tensor_tensor(out=tv, in0=srcv, in1=ss[:, 0:8].unsqueeze(2).to_broadcast([128, 8, 64]),
                                                    op=ALU.mult), reads=[bres, tag + "ss", "TMP"], writes=["TMP"], cost=0.7)
                op("dve", lambda e: e.tensor_tensor(out=dst[:, :, 0:64], in0=tv, in1=gain, op=ALU.mult),
                   reads=["TMP", "GQK"], writes=[dres], cost=0.7)

            def block_stage1(i, par, bi, own, nvalid, zpl_idx):
                x4 = 2 * par + bi
                hk = hcnt[0] % NH
                hcnt[0] += 1
                ht, hr = H32[hk], "H32_%d" % hk
                S.dma("sp", "ldh%d" % hk, lambda e, s: e.dma_start(out=ht, in_=h1s[i]).then_inc(s, 16),
                      reads=["h1s%d" % i], writes=[hr], cost=4.0)
                xmt, xres = XMT[x4], "XMT%d" % x4
                xm, xmres = XMs[bi], "XM%d" % bi
                st, tg = STB[x4], "b%d" % x4
                op("pool", lambda e: e.memset(st[:, 0:1], 0.0), writes=[tg + "ss"])
                op("act", lambda e: e.activation(out=xm, in_=ht, func=AF.Square, accum_out=st[:, 0:1]),
                   reads=[hr], writes=[xmres, tg + "ss"], cost=1.0)
                op("act", lambda e: e.activation(out=st[:, 1:2], in_=st[:, 0:1], func=AF.Ln, scale=1.0 / D, bias=EPS),
                   reads=[tg + "ss"], writes=[tg + "ln"], cost=0.25)
                op("act", lambda e: e.activation(out=st[:, 2:3], in_=st[:, 1:2], func=AF.Exp, scale=-0.5),
                   reads=[tg + "ln"], writes=[tg + "rstd"], cost=0.25)
                op("dve", lambda e: e.scalar_tensor_tensor(out=xm, in0=ht, scalar=st[:, 2:3], in1=GMIX, op0=ALU.mult, op1=ALU.mult),
                   reads=[hr, tg + "rstd", "GMIX"], writes=[xmres], cost=1.3)
                pt = banks[PB_T].bitcast(BF16)

                def tr(e):
                    for c in range(8):
                        ins = e.transpose(out=pt[:, c * 128:(c + 1) * 128], in_=xm[:, c * 128:(c + 1) * 128], identity=ident)
                    return ins
                op("pe", tr, reads=[xmres, "cb"], writes=["bank0"], cost=1.0)
                op("act", lambda e: e.activation(out=xmt, in_=pt.rearrange("p (c t) -> p c t", t=128), func=AF.Copy),
                   reads=["bank0"], writes=[xres], cost=1.1)
                ka, kres = KA[x4], "KA%d" % x4

                def proj(bank, c0, w):
                    def f(e):
                        for c in range(8):
                            ins = e.matmul(bank[:, 0:w], lhsT=xmt[:, c, :], rhs=WM[:, c, c0:c0 + w], start=(c == 0), stop=(c == 7))
                        return ins
                    return f
                op("pe", proj(banks[PB_K], 512, 512), reads=[xres, "WM"], writes=["bank1"], cost=2.5)
                headnorm("k%d" % x4, banks[PB_K], "bank1", ka, kres, GK, 0.0, SSH[x4])
                op("pe", proj(banks[PB_V], 1024, 512), reads=[xres, "WM"], writes=["bank2"], cost=2.5)
                op("act", lambda e: e.activation(out=VALL[:, i, :, 0:64], in_=banks[PB_V][:, 0:512].rearrange("p (h d) -> p h d", d=64),
                                                 func=AF.Copy), reads=["bank2"], writes=["VALL"], cost=0.7)
                if own:
                    op("pe", proj(banks[PB_Q], 0, 512), reads=[xres, "WM"], writes=["bank3"], cost=2.5)
                    headnorm("q%d" % par, banks[PB_Q], "bank3", QAs[par], "QA%d" % par, GQ, float(np.log(0.125)), SSH[4 + par])
                fc = 8 * bi
                yf, yres = YFs[x4], "YF%d" % x4
                sp, spres = SP32[x4], "SP%d" % x4
                op("pe", proj(banks[PB_S][:, fc:fc + 8], 1536, 8), reads=[xres, "WM"], writes=["bank4"], cost=0.5)
                op("dve", lambda e: e.tensor_tensor(out=yf, in0=banks[PB_S][:, fc:fc + 8], in1=af[:, C_BFG:C_BFG + 8], op=ALU.add),
                   reads=["cf"], writes=[yres, "bank4"], cost=0.2)
                op("act", lambda e: e.activation(out=yf, in_=yf, func=AF.Exp, scale=-1.0), reads=[yres], writes=[yres], cost=0.25)
                op("act", lambda e: e.activation(out=sp, in_=yf, func=AF.Ln, bias=1.0), reads=[yres], writes=[spres], cost=0.25)

                def projT(bank, c0, nch):
                    def f(e):
                        for j in range(nch):
                            for c in range(8):
                                ins = e.matmul(bank[:, j * 128:(j + 1) * 128], lhsT=WM[:, c, c0 + j * 128:c0 + (j + 1) * 128],
                                               rhs=xmt[:, c, :], start=(c == 0), stop=(c == 7))
                        return ins
                    return f
                t0_, t1_ = (0, 128) if own else (nvalid - 2, nvalid)

                def projT(bank, c0, nch):
                    def f(e):
                        for j in range(nch):
                            for c in range(8):
                                ins = e.matmul(bank[:, j * 128 + t0_:j * 128 + t1_], lhsT=WM[:, c, c0 + j * 128:c0 + (j + 1) * 128],
                                               rhs=xmt[:, c, t0_:t1_], start=(c == 0), stop=(c == 7))
                        return ins
                    return f
                op("pe", projT(banks[PB_C], 2056, 4), reads=[xres, "WM"], writes=["bank5"], cost=3.0 if own else 2.0)
                op("pe", projT(banks[PB_U], 2568, 4), reads=[xres, "WM"], writes=["bank6"], cost=3.0 if own else 2.0)
                cps = banks[PB_C][:, 0:512].rearrange("p (j t) -> p j t", t=128)
                ups = banks[PB_U][:, 0:512].rearrange("p (j t) -> p j t", t=128)
                if own:
                    zo = ZOs[par]
                    op("act", lambda e: e.activation(out=UC, in_=ups, func=AF.Copy), reads=["bank6"], writes=["UC"], cost=0.7)
                    op("dve", lambda e: e.tensor_tensor(out=zo[:, :, 2:130], in0=cps, in1=UC, op=ALU.mult),
                       reads=["bank5", "UC"], writes=["ZO%d" % par], cost=0.7)
                    op("pe", projT(banks[PB_X], 1544, 4), reads=[xres, "WM"], writes=["bank7"], cost=3.0)
                else:
                    a0 = nvalid - 2
                    zp = ZPL[zpl_idx]
                    op("act", lambda e: e.activation(out=UC[:, :, a0:a0 + 2], in_=ups[:, :, a0:a0 + 2], func=AF.Copy),
                       reads=["bank6"], writes=["UC"], cost=0.2)
                    op("dve", lambda e: e.tensor_tensor(out=zp, in0=cps[:, :, a0:a0 + 2], in1=UC[:, :, a0:a0 + 2], op=ALU.mult),
                       reads=["bank5", "UC"], writes=["ZPL%d" % zpl_idx], cost=0.2)

            def fill_G(x4, gcol, qa, qres):
                ka, kres = KA[x4], "KA%d" % x4
                gp = banks[PB_S][:, gcol:gcol + 8]
                G32 = GRR[x4][:, 0:8]
                R1 = GRR[x4][:, 8:16]
                R2 = GRR[x4][:, 16:24]
                gr = "GRR%d" % x4
                g3 = G32.unsqueeze(2)
                op("act", lambda e: e.activation(out=G32, in_=gp, func=AF.Copy), reads=[], writes=[gr + "g", "bank4"], cost=0.3)
                op("dve", lambda e: e.tensor_copy(out=ka[:, :, 67:68], in_=g3), reads=[gr + "g"], writes=[kres], cost=0.2)
                op("dve", lambda e: e.tensor_tensor(out=R1.unsqueeze(2), in0=g3, in1=ka[:, :, 67:68], op=ALU.subtract),
                   reads=[gr + "g", kres], writes=[gr + "1"], cost=0.2)
                op("dve", lambda e: e.tensor_copy(out=ka[:, :, 68:69], in_=R1.unsqueeze(2)), reads=[gr + "1"], writes=[kres], cost=0.2)
                op("dve", lambda e: e.tensor_tensor(out=R2.unsqueeze(2), in0=R1.unsqueeze(2), in1=ka[:, :, 68:69], op=ALU.subtract),
                   reads=[gr + "1", kres], writes=[gr + "2"], cost=0.2)
                op("dve", lambda e: e.tensor_copy(out=ka[:, :, 69:70], in_=R2.unsqueeze(2)), reads=[gr + "2"], writes=[kres], cost=0.2)
                if qa is not None:
                    op("dve", lambda e: e.tensor_scalar(out=qa[:, :, 64:67], in0=ka[:, :, 67:70], scalar1=-1.0, scalar2=None,
                                                        op0=ALU.mult), reads=[kres], writes=[qres], cost=0.2)

            def aug_out(src, sres, dst_d, col0, dres):
                pt = banks[PB_T].bitcast(BF16)

                def tr(e):
                    for h in range(8):
                        ins = e.transpose(out=pt[0:70, h * 128:(h + 1) * 128], in_=src[:, h, :], identity=ident)
                    return ins
                op("pe", tr, reads=[sres, "cb"], writes=["bank0"], cost=1.0)
                k = stgc[0] % 3
                stgc[0] += 1
                stg = STG[k]
                op("act", lambda e: e.activation(out=stg, in_=pt[0:70, :].rearrange("p (h t) -> p h t", t=128), func=AF.Copy),
                   reads=["bank0"], writes=["STG%d" % k], cost=1.1)
                S.dma("pool", "stg%d" % k, lambda e, s: e.dma_start(out=dst_d[:, :, col0:col0 + 128], in_=stg).then_inc(s, 16),
                      reads=["STG%d" % k], writes=[dres], cost=3.0)

            block_stage1(0, 1, 1, False, 16, 0)
            spm = SP32[3]
            op("pe", lambda e: e.matmul(banks[PB_S][:, 24:32], lhsT=TRI[0:16, :], rhs=spm[0:16, :], start=True, stop=True),
               reads=["SP3", "cf"], writes=["bank4"])
            op("dve", lambda e: e.tensor_copy(out=srun[0:16, :], in_=spm[0:16, :]), reads=["SP3", "SRUN"], writes=["SRUN"])
            fill_G(3, 24, None, None)
            aug_out(KA[3], "KA3", KTd, 0, "KTd")
            zprev = 0
            for n in range(NOWN):
                par = n % 2
                io, ip = 1 + 2 * n, 2 + 2 * n
                zcur = 1 + (n % 3)
                xo, xp = 2 * par, 2 * par + 1
                block_stage1(io, par, 0, True, 128, None)
                block_stage1(ip, par, 1, False, 128, zcur)
                spo, spp = SP32[xo], SP32[xp]
                ro, rp = "SP%d" % xo, "SP%d" % xp

                def gO(e, spo=spo, spp=spp):
                    e.matmul(banks[PB_S][:, 16:24], lhsT=ONES, rhs=srun, start=True, stop=False)
                    e.matmul(banks[PB_S][:, 16:24], lhsT=MA, rhs=spp, start=False, stop=False)
                    return e.matmul(banks[PB_S][:, 16:24], lhsT=TRI, rhs=spo, start=False, stop=True)

                def gP(e, spo=spo, spp=spp):
                    e.matmul(banks[PB_S][:, 24:32], lhsT=ONES, rhs=srun, start=True, stop=False)
                    e.matmul(banks[PB_S][:, 24:32], lhsT=MB, rhs=spo, start=False, stop=False)
                    return e.matmul(banks[PB_S][:, 24:32], lhsT=TRI, rhs=spp, start=False, stop=True)
                op("pe", gO, reads=[ro, rp, "SRUN", "cf"], writes=["bank4"], cost=0.8)
                fill_G(xo, 16, QAs[par], "QA%d" % par)
                op("pe", gP, reads=[ro, rp, "SRUN", "cf"], writes=["bank4"], cost=0.8)
                fill_G(xp, 24, None, None)
                op("dve", lambda e, spo=spo: e.tensor_tensor(out=srun, in0=srun, in1=spo, op=ALU.add), reads=["SRUN", ro], writes=["SRUN"], cost=0.2)
                op("dve", lambda e, spp=spp: e.tensor_tensor(out=srun, in0=srun, in1=spp, op=ALU.add), reads=["SRUN", rp], writes=["SRUN"], cost=0.2)
                aug_out(KA[xo], "KA%d" % xo, KTd, io * 128, "KTd")
                aug_out(QAs[par], "QA%d" % par, QTd, n * 128, "QTd")
                aug_out(KA[xp], "KA%d" % xp, KTd, ip * 128, "KTd")
                zo, zres = ZOs[par], "ZO%d" % par
                zp, zc = ZPL[zprev], ZPL[zcur]
                op("dve", lambda e, zp=zp, zo=zo: e.tensor_scalar(out=zo[:, :, 0:2], in0=zp, scalar1=af[:, C_SEL:C_SEL + 1], scalar2=None,
                                                                  op0=ALU.mult), reads=["ZPL%d" % zprev, "cf", zres], writes=[zres], cost=0.2)
                op("dve", lambda e, zc=zc, zo=zo: e.scalar_tensor_tensor(out=zo[:, :, 0:2], in0=zc, scalar=af[:, C_SEL + 1:C_SEL + 2],
                                                                         in1=zo[:, :, 0:2], op0=ALU.mult, op1=ALU.add),
                   reads=["ZPL%d" % zcur, "cf", zres], writes=[zres], cost=0.2)
                octt = OCTT[n % 2]
                bps = banks[PB_X][:, 0:512].rearrange("p (j t) -> p j t", t=128)
                for j in range(4):
                    yc = YC[j % 2]
                    cw = lambda t, j=j: af[:, C_CW + 3 * j + t:C_CW + 3 * j + t + 1]
                    op("dve", lambda e, j=j, yc=yc, cw=cw, zo=zo: e.tensor_scalar(out=yc, in0=zo[:, j, 0:128], scalar1=cw(0), scalar2=None,
                                                                                 op0=ALU.mult), reads=[zres, "cf"], writes=["YC%d" % (j % 2)], cost=0.3)
                    op("dve", lambda e, j=j, yc=yc, cw=cw, zo=zo: e.scalar_tensor_tensor(out=yc, in0=zo[:, j, 1:129], scalar=cw(1), in1=yc,
                                                                                        op0=ALU.mult, op1=ALU.add),
                       reads=[zres, "cf", "YC%d" % (j % 2)], writes=["YC%d" % (j % 2)], cost=0.35)
                    op("dve", lambda e, j=j, yc=yc, cw=cw, zo=zo: e.scalar_tensor_tensor(out=yc, in0=zo[:, j, 2:130], scalar=cw(2), in1=yc,
                                                                                        op0=ALU.mult, op1=ALU.add),
                       reads=[zres, "cf", "YC%d" % (j % 2)], writes=["YC%d" % (j % 2)], cost=0.35)
                    op("dve", lambda e, j=j, yc=yc: e.tensor_tensor(out=OC32[:, j, :], in0=bps[:, j, :], in1=yc, op=ALU.mult),
                       reads=["bank7", "YC%d" % (j % 2)], writes=["OC32_%d" % j], cost=0.3)
                    op("act", lambda e, j=j, octt=octt: e.activation(out=octt[:, j, :], in_=OC32[:, j, :], func=AF.Copy,
                                                                     scale=af[:, C_CG + j:C_CG + j + 1]),
                       reads=["OC32_%d" % j, "cf"], writes=["OCTT%d" % (n % 2)], cost=0.4)
                op("act", lambda e: e.activation(out=SQ, in_=OC32f, func=AF.Square), reads=["OC32_%d" % j for j in range(4)], writes=["SQ"], cost=0.55)

                def ssm(e):
                    for j in range(4):
                        ins = e.matmul(banks[PB_S][:, 32:33], lhsT=SQ[:, j * 128:(j + 1) * 128], rhs=ONES[:, 0:1],
                                       start=(j == 0), stop=(j == 3))
                    return ins
                op("pe", ssm, reads=["SQ", "cf"], writes=["bank4"], cost=1.8)
                op("act", lambda e, n=n: e.activation(out=ssc[:, n:n + 1], in_=banks[PB_S][:, 32:33], func=AF.Copy),
                   reads=[], writes=["SSC", "bank4"], cost=0.3)
                S.dma("pool", "octt%d" % (n % 2), lambda e, s, octt=octt, n=n: e.dma_start(
                    out=OCTd[n], in_=octt.rearrange("p j t -> p (j t)")).then_inc(s, 16), reads=["OCTT%d" % (n % 2)], writes=["OCTd"])
                zprev = zcur
            if DEBUG:
                S.dma("sp", "dbgV", lambda e, s: e.dma_start(out=Vd, in_=abf[:, 0:VSZ]).then_inc(s, 16), reads=["VALL"])
                S.dma("sp", "dbgS", lambda e, s: e.dma_start(out=sscd, in_=ssc).then_inc(s, 16), reads=["SSC"])
            S.fixed = True
        S.barrier()

        if "C" in PHASES:
            KTH = [abf[0:70, VSZ + 0 + i * 8320:VSZ + 0 + (i + 1) * 8320] for i in range(2)]
            KTF = [abf[:, VSZ + 0 + i * 8320:VSZ + 0 + (i + 1) * 8320] for i in range(2)]
            QTH = [abf[0:70, VSZ + 16640 + i * 4096:VSZ + 16640 + (i + 1) * 4096] for i in range(2)]
            QTF = [abf[:, VSZ + 16640 + i * 4096:VSZ + 16640 + (i + 1) * 4096] for i in range(2)]
            for i_ in range(2):
                op('pool', lambda e, i_=i_: e.memset(KTF[i_][64:128, :], 0.0), writes=['KTH%d' % i_])
                op('pool', lambda e, i_=i_: e.memset(QTF[i_][64:128, :], 0.0), writes=['QTH%d' % i_])
            PT = [abf[:, VSZ + 24832 + i * 1024:VSZ + 24832 + (i + 1) * 1024] for i in range(3)]
            OATT = abf[:, VSZ + 27904:VSZ + 27904 + 16384].rearrange("p (n f) -> p n f", f=512)
            RL = [af[:, F32_FREE + i * 4:F32_FREE + (i + 1) * 4] for i in range(2)]
            units = []
            for h in range(8):
                for c in range(8):
                    lst = [[(0, 0, 16, None)]]
                    for n in range(4 * c):
                        lst.append([(1 + 2 * n, 0, 128, None), (2 + 2 * n, 0, 128, None)])
                    for r in range(4):
                        n = 4 * c + r
                        lst.append([(1 + 2 * n, r * 128, 128, trimask)])
                        lst.append([(2 + 2 * n, r * 128, 128, maskP)])
                    for idx, subs in enumerate(lst):
                        units.append((h, c, subs, idx == 0, idx == len(lst) - 1))

            def load_head(h):
                hb = h % 2
                S.dma("sp", "kth%d" % hb, lambda e, s: e.dma_start(out=KTH[hb], in_=KTd[:, h, :]).then_inc(s, 16),
                      reads=["KTd"], writes=["KTH%d" % hb])
                S.dma("sp", "qth%d" % hb, lambda e, s: e.dma_start(out=QTH[hb], in_=QTd[:, h, :]).then_inc(s, 16),
                      reads=["QTd"], writes=["QTH%d" % hb])

            def s_unit(u, U):
                h, c, subs, first, last = U
                hb = h % 2
                sb = big[u % 2]

                def f(e):
                    for k, (i, c0, kp, mask) in enumerate(subs):
                        o_ = k * 512
                        ins = e.matmul(sb[0:kp, o_ + c0:o_ + 512], lhsT=KTF[hb][:, i * 128:i * 128 + kp],
                                       rhs=QTF[hb][:, c * 512 + c0:(c + 1) * 512], start=True, stop=(mask is None))
                        if mask is not None:
                            ins = e.matmul(sb[0:kp, o_ + c0:o_ + c0 + 128], lhsT=ident[:, 0:kp], rhs=mask, start=False, stop=True)
                    return ins
                op("pe", f, reads=["KTH%d" % hb, "QTH%d" % hb, "cb"], writes=["sbig%d" % (u % 2)])

            def e_unit(u, U):
                h, c, subs, first, last = U
                sb = big[u % 2]
                pt = PT[u % 3]
                if len(subs) == 2:
                    op("act", lambda e: e.activation(out=pt[:, 0:1024], in_=sb[:, 0:1024], func=AF.Exp),
                       reads=["sbig%d" % (u % 2)], writes=["PT%d" % (u % 3)])
                else:
                    i, c0, kp, mask = subs[0]
                    op("act", lambda e: e.activation(out=pt[0:kp, c0:512], in_=sb[0:kp, c0:512], func=AF.Exp),
                       reads=["sbig%d" % (u % 2)], writes=["PT%d" % (u % 3)])

            chunk_no = [0]
            IDF = af[:, C_IDF:C_IDF + 128]
            OTS = [af[0:65, F32_FREE + 64 + i * 512:F32_FREE + 64 + (i + 1) * 512] for i in range(2)]
            pending = []

            def fin_pe(ci, h, c):
                ots = OTS[ci % 2]
                tb = banks[6 + (ci % 2)]
                tbv = tb[:, 0:4 * VD].rearrange("p (m d) -> p m d", d=VD)

                def f(e):
                    for m in range(4):
                        ins = e.matmul(tbv[:, m, 0:65], lhsT=ots[:, m * 128:(m + 1) * 128], rhs=IDF[0:65, 0:65], start=True, stop=True)
                    return ins
                op("pe", f, reads=["OTS%d" % (ci % 2), "cf"], writes=["bank%d" % (6 + ci % 2)])
                rl = RL[ci % 2]
                rr = "RL%d" % (ci % 2)
                op("dve", lambda e: e.reciprocal(out=rl.unsqueeze(2), in_=tbv[:, :, 64:65]), reads=["bank%d" % (6 + ci % 2)], writes=[rr])
                for m in range(4):
                    op("dve", lambda e, m=m: e.tensor_scalar(out=OATT[:, 4 * c + m, h * 64:(h + 1) * 64], in0=tbv[:, m, 0:64],
                                                             scalar1=rl[:, m:m + 1], scalar2=None, op0=ALU.mult),
                       reads=["bank%d" % (6 + ci % 2), rr], writes=["OATT"])

            def pv_unit(u, U):
                h, c, subs, first, last = U
                pt = PT[u % 3]
                ci = chunk_no[0]
                ab = 4 + (ci % 2)
                accT = banks[ab]

                def f(e):
                    for k, (i, c0, kp, mask) in enumerate(subs):
                        voff = (i * 8 + h) * VD
                        ins = e.matmul(accT[:, c0:512], lhsT=abf[0:kp, voff:voff + 128], rhs=pt[0:kp, k * 512 + c0:k * 512 + 512],
                                       start=(first and k == 0), stop=False, skip_group_check=True)
                    return ins
                op("pe", f, reads=["PT%d" % (u % 3), "VALL"], writes=["bank%d" % ab])
                if last:
                    ots = OTS[ci % 2]
                    op("dve", lambda e: e.tensor_copy(out=ots, in_=accT[0:65, 0:512]),
                       reads=["bank%d" % ab], writes=["OTS%d" % (ci % 2)])
                    pending.append([2, ci, h, c])
                    chunk_no[0] += 1

            load_head(0)
            cur_h = -1
            nU = len(units)
            s_unit(0, units[0])
            for u in range(nU):
                U = units[u]
                if U[0] != cur_h:
                    cur_h = U[0]
                    if cur_h + 1 < 8:
                        load_head(cur_h + 1)
                if u + 1 < nU:
                    s_unit(u + 1, units[u + 1])
                e_unit(u, U)
                for p_ in pending:
                    p_[0] -= 1
                while pending and pending[0][0] <= 0:
                    _, ci_, h_, c_ = pending.pop(0)
                    fin_pe(ci_, h_, c_)
                pv_unit(u, U)
            while pending:
                _, ci_, h_, c_ = pending.pop(0)
                fin_pe(ci_, h_, c_)
            S.dma("sp", "oad", lambda e, s: e.dma_start(out=OAd.rearrange("n p f -> p n f"), in_=OATT).then_inc(s, 16),
                  reads=["OATT"], writes=["OAd"])
        S.barrier()

        if "D" in PHASES:
            load_ffn_weights("d", w2i, w2o)
            S.fixed = False
            NB3 = 3
            WMO = abf[:, 67584:67584 + 8192].rearrange("p (j n) -> p j n", n=1024)
            OAt = [abf[:, 75776 + i * 512:75776 + (i + 1) * 512] for i in range(NB3)]
            OAT = [abf[:, 77312 + i * 512:77312 + (i + 1) * 512].rearrange("p (j t) -> p j t", t=128) for i in range(NB3)]
            OCt = [abf[:, 78848 + i * 512:78848 + (i + 1) * 512].rearrange("p (j t) -> p j t", t=128) for i in range(NB3)]
            JK = abf[:, 80384:80384 + 512]

            def f(e, s):
                for j in range(8):
                    e.dma_start(out=WMO[:, j, :], in_=wmo[j * 128:(j + 1) * 128, :]).then_inc(s, 16)
            S.dma("pool", "wmo", f, writes=["WMO"], n=8, cost=15.0)

            def dblock(n):
                k = n % NXT
                b2 = n % NB3
                xt = XT[k]
                xr = "XT%d" % k
                st = STAT[b2]
                stc = STAT[3 + b2]
                S.dma("sp", "ldx%d" % k, lambda e, s: e.dma_start(out=xt, in_=h1s[1 + 2 * n]).then_inc(s, 16), writes=[xr], cost=4.0)
                S.dma("sp", "ldoa%d" % b2, lambda e, s: e.dma_start(out=OAt[b2], in_=OAd[n]).then_inc(s, 16),
                      writes=["OAt%d" % b2])
                S.dma("sp", "ldoc%d" % b2, lambda e, s: e.dma_start(
                    out=OCt[b2].rearrange("p j t -> p (j t)"), in_=OCTd[n]).then_inc(s, 16), writes=["OCt%d" % b2])
                op("pool", lambda e: e.memset(st[:, 0:1], 0.0), writes=["dss%d" % b2])
                op("act", lambda e: e.activation(out=JK, in_=OAt[b2], func=AF.Square, accum_out=st[:, 0:1]),
                   reads=["OAt%d" % b2], writes=["JK", "dss%d" % b2], cost=0.55)
                op("act", lambda e: e.activation(out=st[:, 1:2], in_=st[:, 0:1], func=AF.Ln, scale=1.0 / 512, bias=EPS),
                   reads=["dss%d" % b2], writes=["dln%d" % b2], cost=0.25)
                op("act", lambda e: e.activation(out=st[:, 2:3], in_=st[:, 1:2], func=AF.Exp, scale=-0.5),
                   reads=["dln%d" % b2], writes=["dra%d" % b2], cost=0.25)
                op("act", lambda e: e.activation(out=stc[:, 1:2], in_=ssc[:, n:n + 1], func=AF.Ln, scale=1.0 / 512, bias=EPS),
                   reads=[], writes=["dlc%d" % b2], cost=0.25)
                op("act", lambda e: e.activation(out=stc[:, 2:3], in_=stc[:, 1:2], func=AF.Exp, scale=-0.5),
                   reads=["dlc%d" % b2], writes=["drc%d" % b2], cost=0.25)
                pt = banks[0].bitcast(BF16)

                def tr(e):
                    for j in range(4):
                        ins = e.transpose(out=pt[:, j * 128:(j + 1) * 128], in_=OAt[b2][:, j * 128:(j + 1) * 128], identity=ident)
                    return ins
                op("pe", tr, reads=["OAt%d" % b2], writes=["bank0"], cost=0.5)
                for j in range(4):
                    op("act", lambda e, j=j: e.activation(out=OAT[b2][:, j, :], in_=pt[:, j * 128:(j + 1) * 128], func=AF.Copy,
                                                          scale=af[:, C_AG + j:C_AG + j + 1]),
                       reads=["bank0"], writes=["OAT%d" % b2], cost=0.3)
                for br in range(2):
                    src = OAT[b2] if br == 0 else OCt[b2]
                    sres = ("OAT%d" if br == 0 else "OCt%d") % b2
                    rs = st[:, 2:3] if br == 0 else stc[:, 2:3]
                    rres = ("dra%d" if br == 0 else "drc%d") % b2
                    for hlf in range(2):
                        bk = 1 + 2 * br + hlf

                        def mm(e, src=src, br=br, hlf=hlf, bk=bk):
                            for j in range(4):
                                ins = e.matmul(banks[bk][:, 0:512], lhsT=src[:, j, :], rhs=WMO[:, 4 * br + j, hlf * 512:(hlf + 1) * 512],
                                               start=(j == 0), stop=(j == 3))
                            return ins
                        op("pe", mm, reads=[sres, "WMO"], writes=["bank%d" % bk], cost=1.2)
                        op("dve", lambda e, bk=bk, hlf=hlf, rs=rs: e.scalar_tensor_tensor(
                            out=xt[:, hlf * 512:(hlf + 1) * 512], in0=banks[bk][:, 0:512], scalar=rs,
                            in1=xt[:, hlf * 512:(hlf + 1) * 512], op0=ALU.mult, op1=ALU.add),
                           reads=["bank%d" % bk, rres, xr], writes=[xr], cost=0.7)
                S.dma("pool", "stx%d" % k, lambda e, s: e.dma_start(out=h2s[n], in_=xt).then_inc(s, 16),
                      reads=[xr], writes=["h2s%d" % n], cost=4.0)
            for n in range(NOWN):
                dblock(n)
            S.fixed = True
        S.barrier()

        if "E" in PHASES:
            S.dma("sp", "gainf", lambda e, s: e.dma_start(out=GAIN2, in_=gf).then_inc(s, 16), writes=["GAIN2"])

            def finE(blk, xt, xr, k):
                st = STAT[4 + (k % 2)]
                tg = "e%d" % (k % 2)
                op("pool", lambda e: e.memset(st[:, 0:1], 0.0), writes=[tg + "ss"])
                op("act", lambda e: e.activation(out=JUNK, in_=xt, func=AF.Square, accum_out=st[:, 0:1]),
                   reads=[xr], writes=["JUNK", tg + "ss"])
                op("act", lambda e: e.activation(out=st[:, 1:2], in_=st[:, 0:1], func=AF.Ln, scale=1.0 / D, bias=EPS),
                   reads=[tg + "ss"], writes=[tg + "ln"])
                op("act", lambda e: e.activation(out=st[:, 2:3], in_=st[:, 1:2], func=AF.Exp, scale=-0.5),
                   reads=[tg + "ln"], writes=[tg + "rstd"])
                op("dve", lambda e: e.scalar_tensor_tensor(out=xt, in0=xt, scalar=st[:, 2:3], in1=GAIN2, op0=ALU.mult, op1=ALU.mult),
                   reads=[xr, tg + "rstd", "GAIN2"], writes=[xr])
                S.dma("pool", "stx%d" % k, lambda e, s: e.dma_start(out=out_d[blk], in_=xt).then_inc(s, 16), reads=[xr], writes=["out%d" % blk])
            groups = [[2 * g + j for j in range(2)] for g in range(16)]
            ffn_phase("e", groups, h2s, g2, finE)
        S.barrier()
        run_sched(nc, S, es)
    return nc


def make_inputs(c, x, meta_tokens, ffn1_norm, ffn1_w_in, ffn1_w_out, mix_norm, w_mix_in, b_forget, q_norm, k_norm,
                conv_w, attn_out_norm, conv_out_norm, w_mix_out, ffn2_norm, ffn2_w_in, ffn2_w_out, final_norm):
    b, r = c // 2, c % 2
    f32 = np.float32
    xb = x[b].reshape(64, 128, D)
    xs = np.zeros((NBLK, 128, D), f32)
    xs[0, 0:16] = meta_tokens
    xs[1::2] = xb[r::2]
    xs[2::2] = xb[1 - r::2]
    bc = lambda v: np.ascontiguousarray(np.broadcast_to(np.asarray(v, f32).reshape(1, -1), (128, np.asarray(v).size)))
    cfm = np.zeros((128, NCF), f32)
    s_ = np.arange(128)
    cfm[:, C_TRI:C_TRI + 128] = (s_[:, None] <= s_[None, :]).astype(f32)
    cfm[:, C_ONES:C_ONES + 128] = 1.0
    cfm[:, C_MA:C_MA + 128] = 1.0 if r == 1 else 0.0
    cfm[:, C_MB:C_MB + 128] = 1.0 if r == 0 else 0.0
    cfm[:, C_SEL] = 1.0 if r == 0 else 0.0
    cfm[:, C_SEL + 1] = 1.0 if r == 1 else 0.0
    cfm[:, C_BFG:C_BFG + 8] = np.asarray(b_forget, f32).reshape(1, 8)
    cw = np.asarray(conv_w, f32).reshape(3, 4, 128)
    cfm[:, C_CW:C_CW + 12] = cw.transpose(2, 1, 0).reshape(128, 12)
    cfm[:, C_AG:C_AG + 4] = np.asarray(attn_out_norm, f32).reshape(4, 128).T
    cfm[:, C_CG:C_CG + 4] = np.asarray(conv_out_norm, f32).reshape(4, 128).T
    cfm[:, C_IDF:C_IDF + 128] = np.eye(128, dtype=f32)
    cbm = np.zeros((128, 384), f32)
    cbm[:, 0:128] = np.eye(128, dtype=f32)
    cbm[:, 128:256] = np.where(s_[:, None] <= s_[None, :], 0.0, NEG)
    cbm[:, 256:384] = 0.0 if r == 1 else NEG
    gqk = np.concatenate([np.tile(np.asarray(q_norm, f32).reshape(-1), 8), np.tile(np.asarray(k_norm, f32).reshape(-1), 8)])
    return {
        "xs": xs,
        "w1i": np.ascontiguousarray(ffn1_w_in[0]), "w1o": np.ascontiguousarray(ffn1_w_out[0]),
        "w2i": np.ascontiguousarray(ffn2_w_in[0]), "w2o": np.ascontiguousarray(ffn2_w_out[0]),
        "wmi": np.ascontiguousarray(w_mix_in[0]), "wmo": np.ascontiguousarray(w_mix_out[0]),
        "g1": bc(ffn1_norm), "gm": bc(mix_norm), "g2": bc(ffn2_norm), "gf": bc(final_norm),
        "gqk": bc(gqk), "cf": cfm, "cb": cbm,
    }


_NC_CACHE = {}


def kernel(**inputs):
    inputs = {k: np.asarray(v) for k, v in inputs.items()}
    if "nc" not in _NC_CACHE:
        _NC_CACHE["nc"] = build_nc()
    nc = _NC_CACHE["nc"]
    in_maps = [make_inputs(c, **inputs) for c in range(8)]
    res = run_bass_kernel_spmd(nc, in_maps, core_ids=list(range(8)))
    out = np.zeros((4, 8192, D), np.float32)
    ov = out.reshape(4, 64, 128, D)
    for c in range(8):
        b, r = c // 2, c % 2
        ov[b, r::2] = res.results[c]["out"]
    kernel.last_results = res
    return out
```

```python
import numpy as np
from contextlib import ExitStack
import concourse.bass as bass
import concourse.mybir as mybir
from concourse.bass_utils import run_bass_kernel_spmd

F32 = mybir.dt.float32
BF16 = mybir.dt.bfloat16
AF = mybir.ActivationFunctionType
ALU = mybir.AluOpType
AX = mybir.AxisListType

ENGS = ["pe", "act", "dve", "pool", "sp"]
D = 1024
DFF = 2816
NF = 22
NOWN = 32
NBLK = 65
EPS = 1e-6
NEG = -30000.0
VD = 72
VSZ = NBLK * 8 * VD + 128
DEBUG = False
PHASES = "ABCDE"


class Op:
    __slots__ = ("eng", "fn", "deps", "odeps", "marked", "seq", "key", "total", "idx", "cost", "bgroup")

    def __init__(self, eng, fn):
        self.eng = eng
        self.fn = fn
        self.deps = set()
        self.odeps = set()
        self.marked = False
        self.seq = 0
        self.key = None
        self.total = 0
        self.idx = 0
        self.cost = 0.3
        self.bgroup = None


DEF_COST = {"pe": 0.5, "act": 0.4, "dve": 0.4, "pool": 0.3, "sp": 0.05}


class Sched:
    def __init__(self):
        self.ops = {e: [] for e in ENGS}
        self.last_w = {}
        self.readers = {}
        self.dma_tot = {}
        self.last_dma = {}
        self.n = 0
        self.fixed = True
        self.prev = {}
        self.cur_bar = {}
        self.since_bar = {e: [] for e in ENGS}
        self.bgroups = []

    def _track(self, o, reads, writes):
        deps = o.deps
        for r in reads:
            w = self.last_w.get(r)
            if w is not None:
                deps.add(w)
        for w_ in writes:
            w = self.last_w.get(w_)
            if w is not None:
                deps.add(w)
            for rd in self.readers.get(w_, ()):
                deps.add(rd)
        deps.discard(o)
        for r in reads:
            self.readers.setdefault(r, []).append(o)
        for w_ in writes:
            self.last_w[w_] = o
            self.readers[w_] = []

    def _order(self, o):
        o.idx = self.n
        self.n += 1
        b = self.cur_bar.get(o.eng)
        if b is not None:
            o.odeps.add(b)
        if self.fixed:
            p = self.prev.get(o.eng)
            if p is not None:
                o.odeps.add(p)
        self.prev[o.eng] = o
        self.since_bar[o.eng].append(o)
        self.ops[o.eng].append(o)

    def op(self, eng, fn, reads=(), writes=(), cost=None):
        o = Op(eng, fn)
        o.cost = DEF_COST[eng] if cost is None else cost
        self._track(o, reads, writes)
        self._order(o)
        return o

    def dma(self, eng, key, fn, reads=(), writes=(), n=1, cost=3.0):
        o = Op(eng, fn)
        o.key = key
        o.cost = cost
        self._track(o, reads, writes)
        prev = self.last_dma.get(key)
        if prev is not None:
            o.deps.add(prev)
        self.dma_tot[key] = self.dma_tot.get(key, 0) + 16 * n
        o.total = self.dma_tot[key]
        self.last_dma[key] = o
        self._order(o)
        return o

    def barrier(self):
        dmas = list(self.last_dma.values())
        grp = {}
        for e in ENGS:
            o = Op(e, None)
            o.cost = 0.0
            for d in dmas:
                o.deps.add(d)
            for p in self.since_bar[e]:
                o.odeps.add(p)
            o.idx = self.n
            self.n += 1
            self.ops[e].append(o)
            self.since_bar[e] = []
            self.cur_bar[e] = o
            self.prev[e] = o
            o.bgroup = grp
            grp[e] = o
        self.bgroups.append(grp)
        self.last_w = {}
        self.readers = {}

    def _schedule(self):
        allops = [o for e in ENGS for o in self.ops[e]]
        for grp in self.bgroups:
            for e, b in grp.items():
                for f, bf in grp.items():
                    if f != e:
                        b.odeps |= bf.odeps
        succ = {}
        nun = {}
        for o in allops:
            ds = o.deps | o.odeps
            ds.discard(o)
            nun[o] = len(ds)
            for d in ds:
                succ.setdefault(d, []).append(o)
        ready = {}
        finish = {}
        avail = {e: [] for e in ENGS}
        for o in allops:
            if nun[o] == 0:
                ready[o] = 0.0
                avail[o.eng].append(o)
        t_eng = {e: 0.0 for e in ENGS}
        order = {e: [] for e in ENGS}
        left = len(allops)
        while left:
            best = None
            for e in ENGS:
                te = t_eng[e]
                for o in avail[e]:
                    st = ready[o] if ready[o] > te else te
                    k = (st, o.idx)
                    if best is None or k < best[0]:
                        best = (k, o)
            (st, _), o = best
            e = o.eng
            avail[e].remove(o)
            if o.key is not None:
                t_eng[e] = st + 0.06
                fin = st + o.cost
            else:
                fin = st + o.cost
                t_eng[e] = fin
            finish[o] = fin
            order[e].append(o)
            left -= 1
            for sname in succ.get(o, ()):
                nun[sname] -= 1
                r = fin + (0.15 if sname.eng != e else 0.0)
                if r > ready.get(sname, 0.0):
                    ready[sname] = r
                if nun[sname] == 0:
                    avail[sname.eng].append(sname)
        self.ops = order
        self.est_time = max(finish.values()) if finish else 0.0

    def finalize(self):
        self._schedule()
        for grp in self.bgroups:
            lasts = []
            for f, bf in grp.items():
                lst = self.ops[f]
                i = lst.index(bf) - 1
                while i >= 0 and lst[i].fn is None:
                    i -= 1
                if i >= 0:
                    lasts.append(lst[i])
            for e, b in grp.items():
                for l in lasts:
                    if l.eng != e:
                        b.deps.add(l)
        for e in ENGS:
            for o in self.ops[e]:
                for d in o.deps:
                    if d.key is None and d.fn is not None:
                        if d.eng == o.eng and d.eng == "pe":
                            continue
                        d.marked = True
        for e in ENGS:
            c = 0
            for o in self.ops[e]:
                if o.key is None and o.marked:
                    c += 1
                    o.seq = c

    def emit(self, eng, h, sems, dsems):
        waited = {}
        for o in self.ops[eng]:
            need = {}
            for d in o.deps:
                if d.key is not None:
                    k = ("d", d.key)
                    v = d.total
                else:
                    if not d.marked:
                        continue
                    if d.eng == eng and eng == "pe":
                        continue
                    k = ("e", d.eng)
                    v = d.seq
                if v > need.get(k, 0):
                    need[k] = v
            for k, v in need.items():
                if waited.get(k, 0) >= v:
                    continue
                waited[k] = v
                s = dsems[k[1]] if k[0] == "d" else sems[k[1]]
                h.wait_ge(s, v)
            if o.fn is None:
                continue
            if o.key is not None:
                o.fn(h, dsems[o.key])
            else:
                ins = o.fn(h)
                if o.marked:
                    ins.then_inc(sems[eng], 1)


def run_sched(nc, S, es):
    S.finalize()
    sems = {e: es.enter_context(nc.semaphore("s_" + e)) for e in ENGS}
    dsems = {k: es.enter_context(nc.semaphore("d_%d" % i)) for i, k in enumerate(S.dma_tot)}
    with nc.Block() as block:
        @block.tensor
        def _(h):
            S.emit("pe", h, sems, dsems)

        @block.scalar
        def _(h):
            S.emit("act", h, sems, dsems)

        @block.vector
        def _(h):
            S.emit("dve", h, sems, dsems)

        @block.gpsimd
        def _(h):
            S.emit("pool", h, sems, dsems)

        @block.sync
        def _(h):
            S.emit("sp", h, sems, dsems)


NBF = 87040
NF32 = 9472
CB = NBF - 384
C_TRI, C_ONES, C_MA, C_MB, C_SEL, C_BFG, C_CW, C_AG, C_CG = 0, 128, 256, 384, 512, 514, 522, 534, 538
C_IDF = 544
NCF = 672
SSC = 672
SRUN = 704
F32_FREE = 736


def build_nc():
    nc = bass.Bass("TRN2", target_bir_lowering=False)
    dt = lambda name, shape, ty=F32: nc.dram_tensor(name, shape, ty, kind="ExternalInput").ap()
    xs = dt("xs", [NBLK, 128, D])
    w1i = dt("w1i", [D, 2 * DFF])
    w1o = dt("w1o", [DFF, D])
    w2i = dt("w2i", [D, 2 * DFF])
    w2o = dt("w2o", [DFF, D])
    wmi = dt("wmi", [D, 3080])
    wmo = dt("wmo", [D, D])
    g1 = dt("g1", [128, D])
    gm = dt("gm", [128, D])
    g2 = dt("g2", [128, D])
    gf = dt("gf", [128, D])
    gqk = dt("gqk", [128, 1024])
    cf = dt("cf", [128, NCF])
    cb = dt("cb", [128, 384])
    out_d = nc.dram_tensor("out", [NOWN, 128, D], F32, kind="ExternalOutput").ap()
    kind = "ExternalOutput" if DEBUG else "Internal"
    scr = lambda name, shape, ty: nc.dram_tensor(name, shape, ty, kind=kind).ap()
    h1s = scr("h1s", [NBLK, 128, D], F32)
    h2s = scr("h2s", [NOWN, 128, D], F32)
    KTd = scr("KTd", [70, 8, NBLK * 128], BF16)
    QTd = scr("QTd", [70, 8, NOWN * 128], BF16)
    OAd = scr("OAd", [NOWN, 128, 512], BF16)
    OCTd = scr("OCTd", [NOWN, 128, 512], BF16)
    if DEBUG:
        Vd = scr("Vd", [128, VSZ], BF16)
        sscd = scr("sscd", [128, 32], F32)

    with ExitStack() as es:
        abf = es.enter_context(nc.sbuf_tensor("abf", [128, NBF], BF16))
        af = es.enter_context(nc.sbuf_tensor("af", [128, NF32], F32))
        banks = [es.enter_context(nc.psum_tensor("pb%d" % i, [128, 512], F32)) for i in range(8)]
        S = Sched()
        op = S.op

        ident = abf[:, CB:CB + 128]
        trimask = abf[:, CB + 128:CB + 256]
        maskP = abf[:, CB + 256:CB + 384]
        cfa = af[:, 0:NCF]
        TRI = af[:, C_TRI:C_TRI + 128]
        ONES = af[:, C_ONES:C_ONES + 128]
        MA = af[:, C_MA:C_MA + 128]
        MB = af[:, C_MB:C_MB + 128]
        ssc = af[:, SSC:SSC + 32]
        srun = af[:, SRUN:SRUN + 8]

        def load_consts():
            S.dma("sp", "cf", lambda e, s: e.dma_start(out=cfa, in_=cf).then_inc(s, 16), writes=["cf"])
            S.dma("pool", "cb", lambda e, s: e.dma_start(out=abf[:, CB:CB + 384], in_=cb).then_inc(s, 16), writes=["cb"])

        def rms_rstd(tag, src, width, junk, stat, extra_bias=None):
            op("pool", lambda e: e.memset(stat[:, 0:1], 0.0), reads=[], writes=[tag + "ss"])
            op("act", lambda e: e.activation(out=junk, in_=src, func=AF.Square, accum_out=stat[:, 0:1]),
               reads=[tag + "src"], writes=[tag + "junk", tag + "ss"])
            op("act", lambda e: e.activation(out=stat[:, 1:2], in_=stat[:, 0:1], func=AF.Ln, scale=1.0 / width, bias=EPS),
               reads=[tag + "ss"], writes=[tag + "ln"])
            op("act", lambda e: e.activation(out=stat[:, 2:3], in_=stat[:, 1:2], func=AF.Exp, scale=-0.5),
               reads=[tag + "ln"], writes=[tag + "rstd"])

        def norm_transpose(tag, xt, xres, gain, gres, xn, xnres, stat, dstT, dres, col0, tb=0):
            op("pool", lambda e: e.memset(stat[:, 0:1], 0.0), writes=[tag + "ss"])
            op("act", lambda e: e.activation(out=xn, in_=xt, func=AF.Square, accum_out=stat[:, 0:1]),
               reads=[xres], writes=[xnres, tag + "ss"])
            op("act", lambda e: e.activation(out=stat[:, 1:2], in_=stat[:, 0:1], func=AF.Ln, scale=1.0 / D, bias=EPS),
               reads=[tag + "ss"], writes=[tag + "ln"])
            op("act", lambda e: e.activation(out=stat[:, 2:3], in_=stat[:, 1:2], func=AF.Exp, scale=-0.5),
               reads=[tag + "ln"], writes=[tag + "rstd"])
            op("dve", lambda e: e.scalar_tensor_tensor(out=xn, in0=xt, scalar=stat[:, 2:3], in1=gain,
                                                       op0=ALU.mult, op1=ALU.mult),
               reads=[xres, tag + "rstd", gres], writes=[xnres])
            pt = banks[tb][:].bitcast(BF16)

            def tr(e):
                for c in range(8):
                    i = e.transpose(out=pt[:, c * 128:(c + 1) * 128], in_=xn[:, c * 128:(c + 1) * 128], identity=ident)
                return i
            op("pe", tr, reads=[xnres, "cb"], writes=["bank%d" % tb])
            op("act", lambda e: e.activation(out=dstT[:, :, col0:col0 + 128],
                                             in_=pt.rearrange("p (c t) -> p c t", t=128), func=AF.Copy),
               reads=["bank%d" % tb], writes=[dres])

        W1 = abf[:, 0:8 * 5632].rearrange("p (c n) -> p c n", n=5632)
        W2 = abf[:, 45056:45056 + NF * D].rearrange("p (f n) -> p f n", n=D)
        HT = abf[:, 67584:67584 + NF * 256].rearrange("p (f n) -> p f n", n=256)
        XNTB = [abf[:, 73216 + i * 2048:73216 + (i + 1) * 2048].rearrange("p (c n) -> p c n", n=256) for i in range(2)]
        XN = [abf[:, 77312 + i * 1024:77312 + (i + 1) * 1024] for i in range(2)]
        JUNK = abf[:, 79360:79360 + 1024]
        NXT = 5
        XT = [af[:, F32_FREE + i * 1024:F32_FREE + (i + 1) * 1024] for i in range(NXT)]
        o_ = F32_FREE + NXT * 1024
        SG = [af[:, o_ + i * 256:o_ + (i + 1) * 256] for i in range(2)]
        o_ += 512
        GAIN = af[:, o_:o_ + 1024]
        GAIN2 = af[:, o_ + 1024:o_ + 2048]
        o_ += 2048
        STAT = [af[:, o_ + i * 4:o_ + (i + 1) * 4] for i in range(8)]
        o_ += 32
        assert o_ <= NF32

        def load_ffn_weights(tag, wi, wo):
            for q in range(4):
                def f(e, s, q=q):
                    for c in range(8):
                        e.dma_start(out=W1[:, c, q * 1408:(q + 1) * 1408],
                                    in_=wi[c * 128:(c + 1) * 128, q * 1408:(q + 1) * 1408]).then_inc(s, 16)
                S.dma("pool", tag + "w1q%d" % q, f, writes=["W1q%d" % q], n=8)
            for hlf in range(2):
                def f(e, s, hlf=hlf):
                    e.dma_start(out=W2[:, hlf * 11:(hlf + 1) * 11, :],
                                in_=wo[hlf * 1408:(hlf + 1) * 1408, :].rearrange("(f p) n -> p f n", p=128)).then_inc(s, 16)
                S.dma("pool", tag + "w2h%d" % hlf, f, writes=["W2h%d" % hlf])

        def ffn_phase(tag, groups, src_d, gain_d, finish):
            S.dma("sp", tag + "gain", lambda e, s: e.dma_start(out=GAIN, in_=gain_d).then_inc(s, 16), writes=["GAIN"])
            cnt = [0]
            info = {}

            def prep_a(g, only=None):
                tiles = info.setdefault(g, [])
                for j, blk in enumerate(groups[g]):
                    if only is not None and j != only:
                        continue
                    k = cnt[0] % NXT
                    cnt[0] += 1
                    xt, xr = XT[k], "XT%d" % k
                    tiles.append((xt, xr, k))
                    S.dma("sp", "ldx%d" % k, lambda e, s, xt=xt, blk=blk: e.dma_start(out=xt, in_=src_d[blk]).then_inc(s, 16),
                          reads=[tag + "src%d" % blk], writes=[xr])
                    st, xn, tg = STAT[j], XN[j], "f%d" % j
                    op("pool", lambda e, st=st: e.memset(st[:, 0:1], 0.0), writes=[tg + "ss"])
                    op("act", lambda e, st=st, xn=xn, xt=xt: e.activation(out=xn, in_=xt, func=AF.Square, accum_out=st[:, 0:1]),
                       reads=[xr], writes=["XN%d" % j, tg + "ss"])
                    op("act", lambda e, st=st: e.activation(out=st[:, 1:2], in_=st[:, 0:1], func=AF.Ln, scale=1.0 / D, bias=EPS),
                       reads=[tg + "ss"], writes=[tg + "ln"])
                    op("act", lambda e, st=st: e.activation(out=st[:, 2:3], in_=st[:, 1:2], func=AF.Exp, scale=-0.5),
                       reads=[tg + "ln"], writes=[tg + "rstd"])
                    op("dve", lambda e, st=st, xn=xn, xt=xt: e.scalar_tensor_tensor(out=xn, in0=xt, scalar=st[:, 2:3], in1=GAIN,
                                                                                   op0=ALU.mult, op1=ALU.mult),
                       reads=[xr, tg + "rstd", "GAIN"], writes=["XN%d" % j])

            def prep_b(g, only=None):
                xb = XNTB[g % 2]
                pt = banks[0][:].bitcast(BF16)
                for j, blk in enumerate(groups[g]):
                    if only is not None and j != only:
                        continue
                    xn = XN[j]

                    def tr(e, xn=xn):
                        for c in range(8):
                            i = e.transpose(out=pt[:, c * 128:(c + 1) * 128], in_=xn[:, c * 128:(c + 1) * 128], identity=ident)
                        return i
                    op("pe", tr, reads=["XN%d" % j, "cb"], writes=["bank0"])
                    op("act", lambda e, j=j: e.activation(out=xb[:, :, j * 128:(j + 1) * 128],
                                                          in_=pt.rearrange("p (c t) -> p c t", t=128), func=AF.Copy),
                       reads=["bank0"], writes=["XNT%d" % (g % 2)])

            prep_a(0)
            prep_b(0)
            for g, grp in enumerate(groups):
                N = 128 * len(grp)
                XNT = XNTB[g % 2]
                xres = "XNT%d" % (g % 2)
                tiles = info[g]
                for f in range(NF):
                    pg = banks[1 + 2 * (f % 2)]
                    pu = banks[2 + 2 * (f % 2)]
                    gq = (f * 128) // 1408
                    uq = (DFF + f * 128) // 1408

                    def mmg(e, f=f, pg=pg, N=N, XNT=XNT):
                        for c in range(8):
                            i = e.matmul(pg[:, 0:N], lhsT=W1[:, c, f * 128:(f + 1) * 128], rhs=XNT[:, c, 0:N],
                                         start=(c == 0), stop=(c == 7))
                        return i

                    def mmu(e, f=f, pu=pu, N=N, XNT=XNT):
                        for c in range(8):
                            i = e.matmul(pu[:, 0:N], lhsT=W1[:, c, DFF + f * 128:DFF + (f + 1) * 128], rhs=XNT[:, c, 0:N],
                                         start=(c == 0), stop=(c == 7))
                        return i
                    op("pe", mmg, reads=["W1q%d" % gq, xres], writes=["bank%d" % (1 + 2 * (f % 2))])
                    op("pe", mmu, reads=["W1q%d" % uq, xres], writes=["bank%d" % (2 + 2 * (f % 2))])
                    sg = SG[f % 2]
                    op("act", lambda e, sg=sg, pg=pg, N=N: e.activation(out=sg[:, 0:N], in_=pg[:, 0:N], func=AF.Silu),
                       reads=["bank%d" % (1 + 2 * (f % 2))], writes=["SG%d" % (f % 2)])
                    op("dve", lambda e, sg=sg, pu=pu, f=f, N=N: e.tensor_tensor(out=HT[:, f, 0:N], in0=pu[:, 0:N], in1=sg[:, 0:N],
                                                                                 op=ALU.mult),
                       reads=["bank%d" % (2 + 2 * (f % 2)), "SG%d" % (f % 2)], writes=["HT%d" % f])
                    if f == 4 and g + 1 < len(groups):
                        prep_a(g + 1, 0)
                    if f == 12 and g + 1 < len(groups) and len(groups[g + 1]) > 1:
                        prep_a(g + 1, 1)
                for j, blk in enumerate(grp):
                    xt, xr, k = tiles[j]
                    for hlf in range(2):
                        po = banks[5 + hlf]

                        def mm2a(e, j=j, hlf=hlf, po=po):
                            for f in range(0, 16):
                                i = e.matmul(po[:, 0:512], lhsT=HT[:, f, j * 128:(j + 1) * 128],
                                             rhs=W2[:, f, hlf * 512:(hlf + 1) * 512], start=(f == 0), stop=False)
                            return i

                        def mm2b(e, j=j, hlf=hlf, po=po):
                            for f in range(16, NF):
                                i = e.matmul(po[:, 0:512], lhsT=HT[:, f, j * 128:(j + 1) * 128],
                                             rhs=W2[:, f, hlf * 512:(hlf + 1) * 512], start=False, stop=(f == NF - 1))
                            return i
                        op("pe", mm2a, reads=["HT%d" % f for f in range(16)] + ["W2h0", "W2h1"], writes=["bank%d" % (5 + hlf)])
                        op("pe", mm2b, reads=["HT%d" % f for f in range(16, NF)] + ["W2h0", "W2h1"], writes=["bank%d" % (5 + hlf)])
                        if j == 0 and g + 1 < len(groups) and hlf < len(groups[g + 1]):
                            prep_b(g + 1, hlf)
                        op("dve", lambda e, xt=xt, po=po, hlf=hlf: e.scalar_tensor_tensor(
                            out=xt[:, hlf * 512:(hlf + 1) * 512], in0=po[:, 0:512], scalar=0.5,
                            in1=xt[:, hlf * 512:(hlf + 1) * 512], op0=ALU.mult, op1=ALU.add),
                           reads=["bank%d" % (5 + hlf), xr], writes=[xr])
                    finish(blk, xt, xr, k)
                if len(grp) == 1 and g + 1 < len(groups):
                    pass

        load_consts()
        if "A" in PHASES:
            load_ffn_weights("a", w1i, w1o)

            def finA(blk, xt, xr, k):
                S.dma("pool", "stx%d" % k, lambda e, s: e.dma_start(out=h1s[blk], in_=xt).then_inc(s, 16),
                      reads=[xr], writes=["h1s%d" % blk])
            groups = [[1 + 2 * g + j for j in range(2)] for g in range(32)] + [[0]]
            ffn_phase("a", groups, xs, g1, finA)
        S.barrier()

        VALL = abf[:, 0:VSZ - 128].rearrange("p (i h d) -> p i h d", h=8, d=VD)
        if "B" in PHASES:
            S.fixed = False
            bo = VSZ
            WM = abf[:, bo:bo + 8 * 3080].rearrange("p (c n) -> p c n", n=3080)
            bo += 8 * 3080
            XMs = [abf[:, bo + i * 1024:bo + (i + 1) * 1024] for i in range(2)]
            bo += 2048
            XMT = [abf[:, bo + i * 1024:bo + (i + 1) * 1024].rearrange("p (c n) -> p c n", n=128) for i in range(4)]
            bo += 4096
            KA = [abf[:, bo + i * 560:bo + (i + 1) * 560].rearrange("p (h d) -> p h d", d=70) for i in range(4)]
            bo += 4 * 560
            QAs = [abf[:, bo + i * 560:bo + (i + 1) * 560].rearrange("p (h d) -> p h d", d=70) for i in range(2)]
            bo += 2 * 560
            STG = [abf[0:70, bo + i * 1024:bo + (i + 1) * 1024].rearrange("p (h t) -> p h t", t=128) for i in range(3)]
            bo += 3 * 1024
            OCTT = [abf[:, bo + i * 512:bo + (i + 1) * 512].rearrange("p (j t) -> p j t", t=128) for i in range(2)]
            bo += 1024
            assert bo <= CB
            o = F32_FREE
            NH = 3
            H32 = [af[:, o + i * 1024:o + (i + 1) * 1024] for i in range(NH)]
            o += NH * 1024
            GMIX = af[:, o:o + 1024]
            o += 1024
            GQ = af[:, o:o + 512].rearrange("p (h d) -> p h d", d=64)
            GK = af[:, o + 512:o + 1024].rearrange("p (h d) -> p h d", d=64)
            GQK = af[:, o:o + 1024]
            o += 1024
            TMP = af[:, o:o + 512]
            o += 512
            ZOs = [af[:, o + i * 520:o + (i + 1) * 520].rearrange("p (j t) -> p j t", t=130) for i in range(2)]
            o += 1040
            ZPL = [af[:, o + i * 8:o + (i + 1) * 8].rearrange("p (j t) -> p j t", t=2) for i in range(4)]
            o += 32
            UC = af[:, o:o + 512].rearrange("p (j t) -> p j t", t=128)
            o += 512
            YC = [af[:, o + i * 128:o + (i + 1) * 128] for i in range(2)]
            o += 256
            OC32 = af[:, o:o + 512].rearrange("p (j t) -> p j t", t=128)
            OC32f = af[:, o:o + 512]
            o += 512
            SQ = af[:, o:o + 512]
            o += 512
            STB = [af[:, o + i * 4:o + (i + 1) * 4] for i in range(4)]
            o += 16
            SSH = [af[:, o + i * 8:o + (i + 1) * 8] for i in range(6)]
            o += 48
            SP32 = [af[:, o + i * 8:o + (i + 1) * 8] for i in range(4)]
            o += 32
            YFs = [af[:, o + i * 8:o + (i + 1) * 8] for i in range(4)]
            o += 32
            GRR = [af[:, o + i * 24:o + (i + 1) * 24] for i in range(4)]
            o += 96
            assert o <= NF32, o

            def f(e, s):
                for c in range(8):
                    e.dma_start(out=WM[:, c, :], in_=wmi[c * 128:(c + 1) * 128, :]).then_inc(s, 16)
            S.dma("pool", "wm", f, writes=["WM"], n=8, cost=40.0)
            S.dma("sp", "gmix", lambda e, s: e.dma_start(out=GMIX, in_=gm).then_inc(s, 16), writes=["GMIX"])
            S.dma("sp", "gqk", lambda e, s: e.dma_start(out=GQK, in_=gqk).then_inc(s, 16), writes=["GQK"])
            for i in range(4):
                op("pool", lambda e, i=i: e.memset(KA[i][:, :, 64:67], 1.0), writes=["KA%d" % i])
            for i in range(2):
                op("pool", lambda e, i=i: e.memset(QAs[i][:, :, 67:70], 1.0), writes=["QA%d" % i])
            op("pool", lambda e: e.memset(VALL[:, :, :, 64:65], 1.0), writes=["VALL"])
            op("pool", lambda e: e.memset(srun, 0.0), writes=["SRUN"])
            op("pool", lambda e: e.memset(ssc, 0.0), writes=["SSC"])

            stgc = [0]
            hcnt = [0]
            PB_T, PB_K, PB_V, PB_Q, PB_S, PB_C, PB_U, PB_X = 0, 1, 2, 3, 4, 5, 6, 7

            def headnorm(tag, src_ps, bres, dst, dres, gain, extra_ln_bias, ss):
                srcv = src_ps[:, 0:512].rearrange("p (h d) -> p h d", d=64)
                tv = TMP.rearrange("p (h d) -> p h d", d=64)
                op("act", lambda e: e.activation(out=TMP, in_=src_ps[:, 0:512], func=AF.Square),
                   reads=[bres], writes=["TMP"], cost=0.6)
                op("dve", lambda e: e.tensor_reduce(out=ss[:, 0:8], in_=tv, axis=AX.X, op=ALU.add),
                   reads=["TMP"], writes=[tag + "ss"], cost=0.7)
                op("act", lambda e: e.activation(out=ss[:, 0:8], in_=ss[:, 0:8], func=AF.Ln, scale=1.0 / 64, bias=EPS),
                   reads=[tag + "ss"], writes=[tag + "ss"], cost=0.25)
                op("act", lambda e: e.activation(out=ss[:, 0:8], in_=ss[:, 0:8], func=AF.Exp, scale=-0.5, bias=extra_ln_bias),
                   reads=[tag + "ss"], writes=[tag + "ss"], cost=0.25)
                op("dve", lambda e: e.tensor_tensor(out=tv, in0=srcv, in1=ss[:, 0:8].unsqueeze(2).to_broadcast([128, 8, 64]),
                                                    op=ALU.mult), reads=[bres, tag + "ss", "TMP"], writes=["TMP"], cost=0.7)
                op("dve", lambda e: e.tensor_tensor(out=dst[:, :, 0:64], in0=tv, in1=gain, op=ALU.mult),
                   reads=["TMP", "GQK"], writes=[dres], cost=0.7)

            def block_stage1(i, par, bi, own, nvalid, zpl_idx):
                x4 = 2 * par + bi
                hk = hcnt[0] % NH
                hcnt[0] += 1
                ht, hr = H32[hk], "H32_%d" % hk
                S.dma("sp", "ldh%d" % hk, lambda e, s: e.dma_start(out=ht, in_=h1s[i]).then_inc(s, 16),
                      reads=["h1s%d" % i], writes=[hr], cost=4.0)
                xmt, xres = XMT[x4], "XMT%d" % x4
                xm, xmres = XMs[bi], "XM%d" % bi
                st, tg = STB[x4], "b%d" % x4
                op("pool", lambda e: e.memset(st[:, 0:1], 0.0), writes=[tg + "ss"])
                op("act", lambda e: e.activation(out=xm, in_=ht, func=AF.Square, accum_out=st[:, 0:1]),
                   reads=[hr], writes=[xmres, tg + "ss"], cost=1.0)
                op("act", lambda e: e.activation(out=st[:, 1:2], in_=st[:, 0:1], func=AF.Ln, scale=1.0 / D, bias=EPS),
                   reads=[tg + "ss"], writes=[tg + "ln"], cost=0.25)
                op("act", lambda e: e.activation(out=st[:, 2:3], in_=st[:, 1:2], func=AF.Exp, scale=-0.5),
                   reads=[tg + "ln"], writes=[tg + "rstd"], cost=0.25)
                op("dve", lambda e: e.scalar_tensor_tensor(out=xm, in0=ht, scalar=st[:, 2:3], in1=GMIX, op0=ALU.mult, op1=ALU.mult),
                   reads=[hr, tg + "rstd", "GMIX"], writes=[xmres], cost=1.3)
                pt = banks[PB_T][:].bitcast(BF16)

                def tr(e):
                    for c in range(8):
                        ins = e.transpose(out=pt[:, c * 128:(c + 1) * 128], in_=xm[:, c * 128:(c + 1) * 128], identity=ident)
                    return ins
                op("pe", tr, reads=[xmres, "cb"], writes=["bank0"], cost=1.0)
                op("act", lambda e: e.activation(out=xmt, in_=pt.rearrange("p (c t) -> p c t", t=128), func=AF.Copy),
                   reads=["bank0"], writes=[xres], cost=1.1)
                ka, kres = KA[x4], "KA%d" % x4

                def proj(bank, c0, w):
                    def f(e):
                        for c in range(8):
                            ins = e.matmul(bank[:, 0:w], lhsT=xmt[:, c, :], rhs=WM[:, c, c0:c0 + w], start=(c == 0), stop=(c == 7))
                        return ins
                    return f
                op("pe", proj(banks[PB_K], 512, 512), reads=[xres, "WM"], writes=["bank1"], cost=2.5)
                headnorm("k%d" % x4, banks[PB_K], "bank1", ka, kres, GK, 0.0, SSH[x4])
                op("pe", proj(banks[PB_V], 1024, 512), reads=[xres, "WM"], writes=["bank2"], cost=2.5)
                op("act", lambda e: e.activation(out=VALL[:, i, :, 0:64], in_=banks[PB_V][:, 0:512].rearrange("p (h d) -> p h d", d=64),
                                                 func=AF.Copy), reads=["bank2"], writes=["VALL"], cost=0.7)
                if own:
                    op("pe", proj(banks[PB_Q], 0, 512), reads=[xres, "WM"], writes=["bank3"], cost=2.5)
                    headnorm("q%d" % par, banks[PB_Q], "bank3", QAs[par], "QA%d" % par, GQ, float(np.log(0.125)), SSH[4 + par])
                fc = 8 * bi
                yf, yres = YFs[x4], "YF%d" % x4
                sp, spres = SP32[x4], "SP%d" % x4
                op("pe", proj(banks[PB_S][:, fc:fc + 8], 1536, 8), reads=[xres, "WM"], writes=["bank4"], cost=0.5)
                op("dve", lambda e: e.tensor_tensor(out=yf, in0=banks[PB_S][:, fc:fc + 8], in1=af[:, C_BFG:C_BFG + 8], op=ALU.add),
                   reads=["cf"], writes=[yres, "bank4"], cost=0.2)
                op("act", lambda e: e.activation(out=yf, in_=yf, func=AF.Exp, scale=-1.0), reads=[yres], writes=[yres], cost=0.25)
                op("act", lambda e: e.activation(out=sp, in_=yf, func=AF.Ln, bias=1.0), reads=[yres], writes=[spres], cost=0.25)

                def projT(bank, c0, nch):
                    def f(e):
                        for j in range(nch):
                            for c in range(8):
                                ins = e.matmul(bank[:, j * 128:(j + 1) * 128], lhsT=WM[:, c, c0 + j * 128:c0 + (j + 1) * 128],
                                               rhs=xmt[:, c, :], start=(c == 0), stop=(c == 7))
                        return ins
                    return f
                t0_, t1_ = (0, 128) if own else (nvalid - 2, nvalid)

                def projT(bank, c0, nch):
                    def f(e):
                        for j in range(nch):
                            for c in range(8):
                                ins = e.matmul(bank[:, j * 128 + t0_:j * 128 + t1_], lhsT=WM[:, c, c0 + j * 128:c0 + (j + 1) * 128],
                                               rhs=xmt[:, c, t0_:t1_], start=(c == 0), stop=(c == 7))
                        return ins
                    return f
                op("pe", projT(banks[PB_C], 2056, 4), reads=[xres, "WM"], writes=["bank5"], cost=3.0 if own else 2.0)
                op("pe", projT(banks[PB_U], 2568, 4), reads=[xres, "WM"], writes=["bank6"], cost=3.0 if own else 2.0)
                cps = banks[PB_C][:, 0:512].rearrange("p (j t) -> p j t", t=128)
                ups = banks[PB_U][:, 0:512].rearrange("p (j t) -> p j t", t=128)
                if own:
                    zo = ZOs[par]
                    op("act", lambda e: e.activation(out=UC, in_=ups, func=AF.Copy), reads=["bank6"], writes=["UC"], cost=0.7)
                    op("dve", lambda e: e.tensor_tensor(out=zo[:, :, 2:130], in0=cps, in1=UC, op=ALU.mult),
                       reads=["bank5", "UC"], writes=["ZO%d" % par], cost=0.7)
                    op("pe", projT(banks[PB_X], 1544, 4), reads=[xres, "WM"], writes=["bank7"], cost=3.0)
                else:
                    a0 = nvalid - 2
                    zp = ZPL[zpl_idx]
                    op("act", lambda e: e.activation(out=UC[:, :, a0:a0 + 2], in_=ups[:, :, a0:a0 + 2], func=AF.Copy),
                       reads=["bank6"], writes=["UC"], cost=0.2)
                    op("dve", lambda e: e.tensor_tensor(out=zp, in0=cps[:, :, a0:a0 + 2], in1=UC[:, :, a0:a0 + 2], op=ALU.mult),
                       reads=["bank5", "UC"], writes=["ZPL%d" % zpl_idx], cost=0.2)

            def fill_G(x4, gcol, qa, qres):
                ka, kres = KA[x4], "KA%d" % x4
                gp = banks[PB_S][:, gcol:gcol + 8]
                G32 = GRR[x4][:, 0:8]
                R1 = GRR[x4][:, 8:16]
                R2 = GRR[x4][:, 16:24]
                gr = "GRR%d" % x4
                g3 = G32.unsqueeze(2)
                op("act", lambda e: e.activation(out=G32, in_=gp, func=AF.Copy), reads=[], writes=[gr + "g", "bank4"], cost=0.3)
                op("dve", lambda e: e.tensor_copy(out=ka[:, :, 67:68], in_=g3), reads=[gr + "g"], writes=[kres], cost=0.2)
                op("dve", lambda e: e.tensor_tensor(out=R1.unsqueeze(2), in0=g3, in1=ka[:, :, 67:68], op=ALU.subtract),
                   reads=[gr + "g", kres], writes=[gr + "1"], cost=0.2)
                op("dve", lambda e: e.tensor_copy(out=ka[:, :, 68:69], in_=R1.unsqueeze(2)), reads=[gr + "1"], writes=[kres], cost=0.2)
                op("dve", lambda e: e.tensor_tensor(out=R2.unsqueeze(2), in0=R1.unsqueeze(2), in1=ka[:, :, 68:69], op=ALU.subtract),
                   reads=[gr + "1", kres], writes=[gr + "2"], cost=0.2)
                op("dve", lambda e: e.tensor_copy(out=ka[:, :, 69:70], in_=R2.unsqueeze(2)), reads=[gr + "2"], writes=[kres], cost=0.2)
                if qa is not None:
                    op("dve", lambda e: e.tensor_scalar(out=qa[:, :, 64:67], in0=ka[:, :, 67:70], scalar1=-1.0, scalar2=None,
                                                        op0=ALU.mult), reads=[kres], writes=[qres], cost=0.2)

            def aug_out(src, sres, dst_d, col0, dres):
                pt = banks[PB_T][:].bitcast(BF16)

                def tr(e):
                    for h in range(8):
                        ins = e.transpose(out=pt[0:70, h * 128:(h + 1) * 128], in_=src[:, h, :], identity=ident)
                    return ins
                op("pe", tr, reads=[sres, "cb"], writes=["bank0"], cost=1.0)
                k = stgc[0] % 3
                stgc[0] += 1
                stg = STG[k]
                op("act", lambda e: e.activation(out=stg, in_=pt[0:70, :].rearrange("p (h t) -> p h t", t=128), func=AF.Copy),
                   reads=["bank0"], writes=["STG%d" % k], cost=1.1)
                S.dma("pool", "stg%d" % k, lambda e, s: e.dma_start(out=dst_d[:, :, col0:col0 + 128], in_=stg).then_inc(s, 16),
                      reads=["STG%d" % k], writes=[dres], cost=3.0)

            block_stage1(0, 1, 1, False, 16, 0)
            spm = SP32[3]
            op("pe", lambda e: e.matmul(banks[PB_S][:, 24:32], lhsT=TRI[0:16, :], rhs=spm[0:16, :], start=True, stop=True),
               reads=["SP3", "cf"], writes=["bank4"])
            op("dve", lambda e: e.tensor_copy(out=srun[0:16, :], in_=spm[0:16, :]), reads=["SP3", "SRUN"], writes=["SRUN"])
            fill_G(3, 24, None, None)
            aug_out(KA[3], "KA3", KTd, 0, "KTd")
            zprev = 0
            for n in range(NOWN):
                par = n % 2
                io, ip = 1 + 2 * n, 2 + 2 * n
                zcur = 1 + (n % 3)
                xo, xp = 2 * par, 2 * par + 1
                block_stage1(io, par, 0, True, 128, None)
                block_stage1(ip, par, 1, False, 128, zcur)
                spo, spp = SP32[xo], SP32[xp]
                ro, rp = "SP%d" % xo, "SP%d" % xp

                def gO(e, spo=spo, spp=spp):
                    e.matmul(banks[PB_S][:, 16:24], lhsT=ONES, rhs=srun, start=True, stop=False)
                    e.matmul(banks[PB_S][:, 16:24], lhsT=MA, rhs=spp, start=False, stop=False)
                    return e.matmul(banks[PB_S][:, 16:24], lhsT=TRI, rhs=spo, start=False, stop=True)

                def gP(e, spo=spo, spp=spp):
                    e.matmul(banks[PB_S][:, 24:32], lhsT=ONES, rhs=srun, start=True, stop=False)
                    e.matmul(banks[PB_S][:, 24:32], lhsT=MB, rhs=spo, start=False, stop=False)
                    return e.matmul(banks[PB_S][:, 24:32], lhsT=TRI, rhs=spp, start=False, stop=True)
                op("pe", gO, reads=[ro, rp, "SRUN", "cf"], writes=["bank4"], cost=0.8)
                fill_G(xo, 16, QAs[par], "QA%d" % par)
                op("pe", gP, reads=[ro, rp, "SRUN", "cf"], writes=["bank4"], cost=0.8)
                fill_G(xp, 24, None, None)
                op("dve", lambda e, spo=spo: e.tensor_tensor(out=srun, in0=srun, in1=spo, op=ALU.add), reads=["SRUN", ro], writes=["SRUN"], cost=0.2)
                op("dve", lambda e, spp=spp: e.tensor_tensor(out=srun, in0=srun, in1=spp, op=ALU.add), reads=["SRUN", rp], writes=["SRUN"], cost=0.2)
                aug_out(KA[xo], "KA%d" % xo, KTd, io * 128, "KTd")
                aug_out(QAs[par], "QA%d" % par, QTd, n * 128, "QTd")
                aug_out(KA[xp], "KA%d" % xp, KTd, ip * 128, "KTd")
                zo, zres = ZOs[par], "ZO%d" % par
                zp, zc = ZPL[zprev], ZPL[zcur]
                op("dve", lambda e, zp=zp, zo=zo: e.tensor_scalar(out=zo[:, :, 0:2], in0=zp, scalar1=af[:, C_SEL:C_SEL + 1], scalar2=None,
                                                                  op0=ALU.mult), reads=["ZPL%d" % zprev, "cf", zres], writes=[zres], cost=0.2)
                op("dve", lambda e, zc=zc, zo=zo: e.scalar_tensor_tensor(out=zo[:, :, 0:2], in0=zc, scalar=af[:, C_SEL + 1:C_SEL + 2],
                                                                         in1=zo[:, :, 0:2], op0=ALU.mult, op1=ALU.add),
                   reads=["ZPL%d" % zcur, "cf", zres], writes=[zres], cost=0.2)
                octt = OCTT[n % 2]
                bps = banks[PB_X][:, 0:512].rearrange("p (j t) -> p j t", t=128)
                for j in range(4):
                    yc = YC[j % 2]
                    cw = lambda t, j=j: af[:, C_CW + 3 * j + t:C_CW + 3 * j + t + 1]
                    op("dve", lambda e, j=j, yc=yc, cw=cw, zo=zo: e.tensor_scalar(out=yc, in0=zo[:, j, 0:128], scalar1=cw(0), scalar2=None,
                                                                                 op0=ALU.mult), reads=[zres, "cf"], writes=["YC%d" % (j % 2)], cost=0.3)
                    op("dve", lambda e, j=j, yc=yc, cw=cw, zo=zo: e.scalar_tensor_tensor(out=yc, in0=zo[:, j, 1:129], scalar=cw(1), in1=yc,
                                                                                        op0=ALU.mult, op1=ALU.add),
                       reads=[zres, "cf", "YC%d" % (j % 2)], writes=["YC%d" % (j % 2)], cost=0.35)
                    op("dve", lambda e, j=j, yc=yc, cw=cw, zo=zo: e.scalar_tensor_tensor(out=yc, in0=zo[:, j, 2:130], scalar=cw(2), in1=yc,
                                                                                        op0=ALU.mult, op1=ALU.add),
                       reads=[zres, "cf", "YC%d" % (j % 2)], writes=["YC%d" % (j % 2)], cost=0.35)
                    op("dve", lambda e, j=j, yc=yc: e.tensor_tensor(out=OC32[:, j, :], in0=bps[:, j, :], in1=yc, op=ALU.mult),
                       reads=["bank7", "YC%d" % (j % 2)], writes=["OC32_%d" % j], cost=0.3)
                    op("act", lambda e, j=j, octt=octt: e.activation(out=octt[:, j, :], in_=OC32[:, j, :], func=AF.Copy,
                                                                     scale=af[:, C_CG + j:C_CG + j + 1]),
                       reads=["OC32_%d" % j, "cf"], writes=["OCTT%d" % (n % 2)], cost=0.4)
                op("act", lambda e: e.activation(out=SQ, in_=OC32f, func=AF.Square), reads=["OC32_%d" % j for j in range(4)], writes=["SQ"], cost=0.55)

                def ssm(e):
                    for j in range(4):
                        ins = e.matmul(banks[PB_S][:, 32:33], lhsT=SQ[:, j * 128:(j + 1) * 128], rhs=ONES[:, 0:1],
                                       start=(j == 0), stop=(j == 3))
                    return ins
                op("pe", ssm, reads=["SQ", "cf"], writes=["bank4"], cost=1.8)
                op("act", lambda e, n=n: e.activation(out=ssc[:, n:n + 1], in_=banks[PB_S][:, 32:33], func=AF.Copy),
                   reads=[], writes=["SSC", "bank4"], cost=0.3)
                S.dma("pool", "octt%d" % (n % 2), lambda e, s, octt=octt, n=n: e.dma_start(
                    out=OCTd[n], in_=octt.rearrange("p j t -> p (j t)")).then_inc(s, 16), reads=["OCTT%d" % (n % 2)], writes=["OCTd"])
                zprev = zcur
            if DEBUG:
                S.dma("sp", "dbgV", lambda e, s: e.dma_start(out=Vd, in_=abf[:, 0:VSZ]).then_inc(s, 16), reads=["VALL"])
                S.dma("sp", "dbgS", lambda e, s: e.dma_start(out=sscd, in_=ssc).then_inc(s, 16), reads=["SSC"])
            S.fixed = True
        S.barrier()

        if "C" in PHASES:
            KTH = [abf[0:70, VSZ + 0 + i * 8320:VSZ + 0 + (i + 1) * 8320] for i in range(2)]
            KTF = [abf[:, VSZ + 0 + i * 8320:VSZ + 0 + (i + 1) * 8320] for i in range(2)]
            QTH = [abf[0:70, VSZ + 16640 + i * 4096:VSZ + 16640 + (i + 1) * 4096] for i in range(2)]
            QTF = [abf[:, VSZ + 16640 + i * 4096:VSZ + 16640 + (i + 1) * 4096] for i in range(2)]
            for i_ in range(2):
                op('pool', lambda e, i_=i_: e.memset(KTF[i_][64:128, :], 0.0), writes=['KTH%d' % i_])
                op('pool', lambda e, i_=i_: e.memset(QTF[i_][64:128, :], 0.0), writes=['QTH%d' % i_])
            PT = [abf[:, VSZ + 24832 + i * 512:VSZ + 24832 + (i + 1) * 512] for i in range(3)]
            OATT = abf[:, VSZ + 26368:VSZ + 26368 + 16384].rearrange("p (n f) -> p n f", f=512)
            RL = [af[:, F32_FREE + i * 4:F32_FREE + (i + 1) * 4] for i in range(2)]
            units = []
            for h in range(8):
                for c in range(8):
                    lst = [(0, 0, 16, None)]
                    for n in range(4 * c):
                        lst.append((1 + 2 * n, 0, 128, None))
                        lst.append((2 + 2 * n, 0, 128, None))
                    for r in range(4):
                        n = 4 * c + r
                        lst.append((1 + 2 * n, r * 128, 128, trimask))
                        lst.append((2 + 2 * n, r * 128, 128, maskP))
                    for idx, (i, c0, kp, mask) in enumerate(lst):
                        units.append((h, c, i, c0, kp, mask, idx == 0, idx == len(lst) - 1))

            def load_head(h):
                hb = h % 2
                S.dma("sp", "kth%d" % hb, lambda e, s: e.dma_start(out=KTH[hb], in_=KTd[:, h, :]).then_inc(s, 16),
                      reads=["KTd"], writes=["KTH%d" % hb])
                S.dma("sp", "qth%d" % hb, lambda e, s: e.dma_start(out=QTH[hb], in_=QTd[:, h, :]).then_inc(s, 16),
                      reads=["QTd"], writes=["QTH%d" % hb])

            def s_unit(u, U):
                h, c, i, c0, kp, mask, first, last = U
                hb = h % 2
                sb = banks[u % 4]

                def f(e):
                    ins = e.matmul(sb[0:kp, c0:512], lhsT=KTF[hb][:, i * 128:i * 128 + kp],
                                   rhs=QTF[hb][:, c * 512 + c0:(c + 1) * 512], start=True, stop=(mask is None))
                    if mask is not None:
                        ins = e.matmul(sb[0:kp, c0:c0 + 128], lhsT=ident[:, 0:kp], rhs=mask, start=False, stop=True)
                    return ins
                op("pe", f, reads=["KTH%d" % hb, "QTH%d" % hb, "cb"], writes=["bank%d" % (u % 4)])

            def e_unit(u, U):
                h, c, i, c0, kp, mask, first, last = U
                sb = banks[u % 4]
                pt = PT[u % 3]
                op("act", lambda e: e.activation(out=pt[0:kp, c0:512], in_=sb[0:kp, c0:512], func=AF.Exp),
                   reads=["bank%d" % (u % 4)], writes=["PT%d" % (u % 3)])

            chunk_no = [0]
            IDF = af[:, C_IDF:C_IDF + 128]
            OTS = [af[0:65, F32_FREE + 64 + i * 512:F32_FREE + 64 + (i + 1) * 512] for i in range(2)]
            pending = []

            def fin_pe(ci, h, c):
                ots = OTS[ci % 2]
                tb = banks[6 + (ci % 2)]
                tbv = tb[:, 0:4 * VD].rearrange("p (m d) -> p m d", d=VD)

                def f(e):
                    for m in range(4):
                        ins = e.matmul(tbv[:, m, 0:65], lhsT=ots[:, m * 128:(m + 1) * 128], rhs=IDF[0:65, 0:65], start=True, stop=True)
                    return ins
                op("pe", f, reads=["OTS%d" % (ci % 2), "cf"], writes=["bank%d" % (6 + ci % 2)])
                rl = RL[ci % 2]
                rr = "RL%d" % (ci % 2)
                op("dve", lambda e: e.reciprocal(out=rl.unsqueeze(2), in_=tbv[:, :, 64:65]), reads=["bank%d" % (6 + ci % 2)], writes=[rr])
                for m in range(4):
                    op("dve", lambda e, m=m: e.tensor_scalar(out=OATT[:, 4 * c + m, h * 64:(h + 1) * 64], in0=tbv[:, m, 0:64],
                                                             scalar1=rl[:, m:m + 1], scalar2=None, op0=ALU.mult),
                       reads=["bank%d" % (6 + ci % 2), rr], writes=["OATT"])

            def pv_unit(u, U):
                h, c, i, c0, kp, mask, first, last = U
                pt = PT[u % 3]
                ci = chunk_no[0]
                ab = 4 + (ci % 2)
                accT = banks[ab]
                voff = (i * 8 + h) * VD
                op("pe", lambda e: e.matmul(accT[:, c0:512], lhsT=abf[0:kp, voff:voff + 128], rhs=pt[0:kp, c0:512],
                                            start=first, stop=False, skip_group_check=True),
                   reads=["PT%d" % (u % 3), "VALL"], writes=["bank%d" % ab])
                if last:
                    ots = OTS[ci % 2]
                    op("dve", lambda e: e.tensor_copy(out=ots, in_=accT[0:65, 0:512]),
                       reads=["bank%d" % ab], writes=["OTS%d" % (ci % 2)])
                    pending.append([3, ci, h, c])
                    chunk_no[0] += 1

            load_head(0)
            cur_h = -1
            nU = len(units)
            s_unit(0, units[0])
            s_unit(1, units[1])
            for u in range(nU):
                U = units[u]
                if U[0] != cur_h:
                    cur_h = U[0]
                    if cur_h + 1 < 8:
                        load_head(cur_h + 1)
                if u + 2 < nU:
                    s_unit(u + 2, units[u + 2])
                e_unit(u, U)
                for p_ in pending:
                    p_[0] -= 1
                while pending and pending[0][0] <= 0:
                    _, ci_, h_, c_ = pending.pop(0)
                    fin_pe(ci_, h_, c_)
                pv_unit(u, U)
            while pending:
                _, ci_, h_, c_ = pending.pop(0)
                fin_pe(ci_, h_, c_)
            S.dma("sp", "oad", lambda e, s: e.dma_start(out=OAd.rearrange("n p f -> p n f"), in_=OATT).then_inc(s, 16),
                  reads=["OATT"], writes=["OAd"])
        S.barrier()

        if "D" in PHASES:
            load_ffn_weights("d", w2i, w2o)
            S.fixed = False
            NB3 = 3
            WMO = abf[:, 67584:67584 + 8192].rearrange("p (j n) -> p j n", n=1024)
            OAt = [abf[:, 75776 + i * 512:75776 + (i + 1) * 512] for i in range(NB3)]
            OAT = [abf[:, 77312 + i * 512:77312 + (i + 1) * 512].rearrange("p (j t) -> p j t", t=128) for i in range(NB3)]
            OCt = [abf[:, 78848 + i * 512:78848 + (i + 1) * 512].rearrange("p (j t) -> p j t", t=128) for i in range(NB3)]
            JK = abf[:, 80384:80384 + 512]

            def f(e, s):
                for j in range(8):
                    e.dma_start(out=WMO[:, j, :], in_=wmo[j * 128:(j + 1) * 128, :]).then_inc(s, 16)
            S.dma("pool", "wmo", f, writes=["WMO"], n=8, cost=15.0)

            def dblock(n):
                k = n % NXT
                b2 = n % NB3
                xt = XT[k]
                xr = "XT%d" % k
                st = STAT[b2]
                stc = STAT[3 + b2]
                S.dma("sp", "ldx%d" % k, lambda e, s: e.dma_start(out=xt, in_=h1s[1 + 2 * n]).then_inc(s, 16), writes=[xr], cost=4.0)
                S.dma("sp", "ldoa%d" % b2, lambda e, s: e.dma_start(out=OAt[b2], in_=OAd[n]).then_inc(s, 16),
                      writes=["OAt%d" % b2])
                S.dma("sp", "ldoc%d" % b2, lambda e, s: e.dma_start(
                    out=OCt[b2].rearrange("p j t -> p (j t)"), in_=OCTd[n]).then_inc(s, 16), writes=["OCt%d" % b2])
                op("pool", lambda e: e.memset(st[:, 0:1], 0.0), writes=["dss%d" % b2])
                op("act", lambda e: e.activation(out=JK, in_=OAt[b2], func=AF.Square, accum_out=st[:, 0:1]),
                   reads=["OAt%d" % b2], writes=["JK", "dss%d" % b2], cost=0.55)
                op("act", lambda e: e.activation(out=st[:, 1:2], in_=st[:, 0:1], func=AF.Ln, scale=1.0 / 512, bias=EPS),
                   reads=["dss%d" % b2], writes=["dln%d" % b2], cost=0.25)
                op("act", lambda e: e.activation(out=st[:, 2:3], in_=st[:, 1:2], func=AF.Exp, scale=-0.5),
                   reads=["dln%d" % b2], writes=["dra%d" % b2], cost=0.25)
                op("act", lambda e: e.activation(out=stc[:, 1:2], in_=ssc[:, n:n + 1], func=AF.Ln, scale=1.0 / 512, bias=EPS),
                   reads=[], writes=["dlc%d" % b2], cost=0.25)
                op("act", lambda e: e.activation(out=stc[:, 2:3], in_=stc[:, 1:2], func=AF.Exp, scale=-0.5),
                   reads=["dlc%d" % b2], writes=["drc%d" % b2], cost=0.25)
                pt = banks[0][:].bitcast(BF16)

                def tr(e):
                    for j in range(4):
                        ins = e.transpose(out=pt[:, j * 128:(j + 1) * 128], in_=OAt[b2][:, j * 128:(j + 1) * 128], identity=ident)
                    return ins
                op("pe", tr, reads=["OAt%d" % b2], writes=["bank0"], cost=0.5)
                for j in range(4):
                    op("act", lambda e, j=j: e.activation(out=OAT[b2][:, j, :], in_=pt[:, j * 128:(j + 1) * 128], func=AF.Copy,
                                                          scale=af[:, C_AG + j:C_AG + j + 1]),
                       reads=["bank0"], writes=["OAT%d" % b2], cost=0.3)
                for br in range(2):
                    src = OAT[b2] if br == 0 else OCt[b2]
                    sres = ("OAT%d" if br == 0 else "OCt%d") % b2
                    rs = st[:, 2:3] if br == 0 else stc[:, 2:3]
                    rres = ("dra%d" if br == 0 else "drc%d") % b2
                    for hlf in range(2):
                        bk = 1 + 2 * br + hlf

                        def mm(e, src=src, br=br, hlf=hlf, bk=bk):
                            for j in range(4):
                                ins = e.matmul(banks[bk][:, 0:512], lhsT=src[:, j, :], rhs=WMO[:, 4 * br + j, hlf * 512:(hlf + 1) * 512],
                                               start=(j == 0), stop=(j == 3))
                            return ins
                        op("pe", mm, reads=[sres, "WMO"], writes=["bank%d" % bk], cost=1.2)
                        op("dve", lambda e, bk=bk, hlf=hlf, rs=rs: e.scalar_tensor_tensor(
                            out=xt[:, hlf * 512:(hlf + 1) * 512], in0=banks[bk][:, 0:512], scalar=rs,
                            in1=xt[:, hlf * 512:(hlf + 1) * 512], op0=ALU.mult, op1=ALU.add),
                           reads=["bank%d" % bk, rres, xr], writes=[xr], cost=0.7)
                S.dma("pool", "stx%d" % k, lambda e, s: e.dma_start(out=h2s[n], in_=xt).then_inc(s, 16),
                      reads=[xr], writes=["h2s%d" % n], cost=4.0)
            for n in range(NOWN):
                dblock(n)
            S.fixed = True
        S.barrier()

        if "E" in PHASES:
            S.dma("sp", "gainf", lambda e, s: e.dma_start(out=GAIN2, in_=gf).then_inc(s, 16), writes=["GAIN2"])

            def finE(blk, xt, xr, k):
                st = STAT[4 + (k % 2)]
                tg = "e%d" % (k % 2)
                op("pool", lambda e: e.memset(st[:, 0:1], 0.0), writes=[tg + "ss"])
                op("act", lambda e: e.activation(out=JUNK, in_=xt, func=AF.Square, accum_out=st[:, 0:1]),
                   reads=[xr], writes=["JUNK", tg + "ss"])
                op("act", lambda e: e.activation(out=st[:, 1:2], in_=st[:, 0:1], func=AF.Ln, scale=1.0 / D, bias=EPS),
                   reads=[tg + "ss"], writes=[tg + "ln"])
                op("act", lambda e: e.activation(out=st[:, 2:3], in_=st[:, 1:2], func=AF.Exp, scale=-0.5),
                   reads=[tg + "ln"], writes=[tg + "rstd"])
                op("dve", lambda e: e.scalar_tensor_tensor(out=xt, in0=xt, scalar=st[:, 2:3], in1=GAIN2, op0=ALU.mult, op1=ALU.mult),
                   reads=[xr, tg + "rstd", "GAIN2"], writes=[xr])
                S.dma("pool", "stx%d" % k, lambda e, s: e.dma_start(out=out_d[blk], in_=xt).then_inc(s, 16), reads=[xr], writes=["out%d" % blk])
            groups = [[2 * g + j for j in range(2)] for g in range(16)]
            ffn_phase("e", groups, h2s, g2, finE)
        S.barrier()
        run_sched(nc, S, es)
    return nc


def make_inputs(c, x, meta_tokens, ffn1_norm, ffn1_w_in, ffn1_w_out, mix_norm, w_mix_in, b_forget, q_norm, k_norm,
                conv_w, attn_out_norm, conv_out_norm, w_mix_out, ffn2_norm, ffn2_w_in, ffn2_w_out, final_norm):
    b, r = c // 2, c % 2
    f32 = np.float32
    xb = x[b].reshape(64, 128, D)
    xs = np.zeros((NBLK, 128, D), f32)
    xs[0, 0:16] = meta_tokens
    xs[1::2] = xb[r::2]
    xs[2::2] = xb[1 - r::2]
    bc = lambda v: np.ascontiguousarray(np.broadcast_to(np.asarray(v, f32).reshape(1, -1), (128, np.asarray(v).size)))
    cfm = np.zeros((128, NCF), f32)
    s_ = np.arange(128)
    cfm[:, C_TRI:C_TRI + 128] = (s_[:, None] <= s_[None, :]).astype(f32)
    cfm[:, C_ONES:C_ONES + 128] = 1.0
    cfm[:, C_MA:C_MA + 128] = 1.0 if r == 1 else 0.0
    cfm[:, C_MB:C_MB + 128] = 1.0 if r == 0 else 0.0
    cfm[:, C_SEL] = 1.0 if r == 0 else 0.0
    cfm[:, C_SEL + 1] = 1.0 if r == 1 else 0.0
    cfm[:, C_BFG:C_BFG + 8] = np.asarray(b_forget, f32).reshape(1, 8)
    cw = np.asarray(conv_w, f32).reshape(3, 4, 128)
    cfm[:, C_CW:C_CW + 12] = cw.transpose(2, 1, 0).reshape(128, 12)
    cfm[:, C_AG:C_AG + 4] = np.asarray(attn_out_norm, f32).reshape(4, 128).T
    cfm[:, C_CG:C_CG + 4] = np.asarray(conv_out_norm, f32).reshape(4, 128).T
    cfm[:, C_IDF:C_IDF + 128] = np.eye(128, dtype=f32)
    cbm = np.zeros((128, 384), f32)
    cbm[:, 0:128] = np.eye(128, dtype=f32)
    cbm[:, 128:256] = np.where(s_[:, None] <= s_[None, :], 0.0, NEG)
    cbm[:, 256:384] = 0.0 if r == 1 else NEG
    gqk = np.concatenate([np.tile(np.asarray(q_norm, f32).reshape(-1), 8), np.tile(np.asarray(k_norm, f32).reshape(-1), 8)])
    return {
        "xs": xs,
        "w1i": np.ascontiguousarray(ffn1_w_in[0]), "w1o": np.ascontiguousarray(ffn1_w_out[0]),
        "w2i": np.ascontiguousarray(ffn2_w_in[0]), "w2o": np.ascontiguousarray(ffn2_w_out[0]),
        "wmi": np.ascontiguousarray(w_mix_in[0]), "wmo": np.ascontiguousarray(w_mix_out[0]),
        "g1": bc(ffn1_norm), "gm": bc(mix_norm), "g2": bc(ffn2_norm), "gf": bc(final_norm),
        "gqk": bc(gqk), "cf": cfm, "cb": cbm,
    }


_NC_CACHE = {}


def kernel(**inputs):
    inputs = {k: np.asarray(v) for k, v in inputs.items()}
    if "nc" not in _NC_CACHE:
        _NC_CACHE["nc"] = build_nc()
    nc = _NC_CACHE["nc"]
    in_maps = [make_inputs(c, **inputs) for c in range(8)]
    res = run_bass_kernel_spmd(nc, in_maps, core_ids=list(range(8)))
    out = np.zeros((4, 8192, D), np.float32)
    ov = out.reshape(4, 64, 128, D)
    for c in range(8):
        b, r = c // 2, c % 2
        ov[b, r::2] = res.results[c]["out"]
    kernel.last_results = res
    return out
```
